# Optimizing a Trainium2 kernel written in Bass

```python
import jax, jax.numpy as jnp
from jax import lax
import numpy as np

D_MODEL = 1024
BATCH = 8
SEQ = 2048
DEPTH = 1

RET_HEADS = 4
RET_HEAD_DIM = 128
RET_WIDTH = RET_HEADS * RET_HEAD_DIM
RET_CHUNK = 128
ROPE_BASE = 10000.0
SGU_GROUPS = 4
SGU_GROUP_DIM = 128
SGU_WIDTH = SGU_GROUPS * SGU_GROUP_DIM
SGU_CHUNK = 128
MIX_WIDTH = RET_WIDTH + SGU_WIDTH
PROJ_WIDTH = 4 * RET_WIDTH + 2 * SGU_WIDTH
D_FF = 2816
CONV_WIDTH = 3
EPS = 1e-6

kernel_name = "hymba_style_retention_sgu_convffn"


def rmsnorm(x, g):
    xf = x.astype(jnp.float32)
    y = xf * lax.rsqrt(jnp.mean(xf * xf, axis=-1, keepdims=True) + EPS)
    return (y * g.astype(jnp.float32)).astype(x.dtype)


def layernorm(x, g, b):
    xf = x.astype(jnp.float32)
    mu = jnp.mean(xf, axis=-1, keepdims=True)
    xc = xf - mu
    y = xc * lax.rsqrt(jnp.mean(xc * xc, axis=-1, keepdims=True) + EPS)
    return (y * g.astype(jnp.float32) + b.astype(jnp.float32)).astype(x.dtype)


def rotary(x, cos, sin):
    half = x.shape[-1] // 2
    x1, x2 = x[..., :half], x[..., half:]
    c = cos[None, :, None, :]
    s = sin[None, :, None, :]
    return jnp.concatenate([x1 * c - x2 * s, x2 * c + x1 * s], axis=-1)


def retention_chunkwise(q, k, v):
    B, S, H, D = q.shape
    C = RET_CHUNK
    N = S // C
    dt = q.dtype
    log_gamma = jnp.log(1.0 - jnp.power(2.0, -5.0 - jnp.arange(H, dtype=jnp.float32)))
    pos = jnp.arange(C, dtype=jnp.float32)
    diff = pos[:, None] - pos[None, :]
    decay_mask = jnp.where(diff >= 0.0,
                           jnp.exp(log_gamma[:, None, None] * jnp.maximum(diff, 0.0)[None]),
                           0.0).astype(dt)
    k_decay = jnp.exp(log_gamma[:, None] * (C - 1.0 - pos)[None]).astype(dt)
    q_decay = jnp.exp(log_gamma[:, None] * (pos + 1.0)[None]).astype(dt)
    chunk_decay = jnp.exp(log_gamma * C).astype(dt)

    def to_chunks(t):
        return t.reshape(B, N, C, H, D).transpose(0, 3, 1, 2, 4)

    qc, kc, vc = to_chunks(q), to_chunks(k), to_chunks(v)
    scores = jnp.einsum('bhnqd,bhnkd->bhnqk', qc, kc) * decay_mask[None, :, None]
    intra = jnp.einsum('bhnqk,bhnkd->bhnqd', scores, vc)
    kv = jnp.einsum('bhnkd,bhnke->bhnde', kc * k_decay[None, :, None, :, None], vc)

    def step(state, kv_n):
        return state * chunk_decay[None, :, None, None] + kv_n, state

    init = jnp.zeros((B, H, D, D), dtype=kv.dtype)
    _, s_prev = lax.scan(step, init, jnp.moveaxis(kv, 2, 0))
    s_prev = jnp.moveaxis(s_prev, 0, 2)
    cross = jnp.einsum('bhnqd,bhnde->bhnqe', qc * q_decay[None, :, None, :, None], s_prev)
    out = intra + cross
    return out.transpose(0, 2, 3, 1, 4).reshape(B, S, H, D)


def spatial_gating_chunked(u, v, ln_g, ln_b, w_s, b_s):
    B, S, _ = u.shape
    C = SGU_CHUNK
    N = S // C
    G, dg = SGU_GROUPS, SGU_GROUP_DIM
    vn = layernorm(v.reshape(B, N, C, G, dg), ln_g, ln_b)
    causal = jnp.tril(jnp.ones((C, C), dtype=w_s.dtype))
    w = w_s * causal[None]
    mixed = jnp.einsum('gts,bnsgd->bntgd', w, vn) + b_s.T[None, None, :, :, None]
    return u * mixed.reshape(B, S, G * dg)


def causal_depthwise_conv(h, w, b):
    S = h.shape[1]
    hp = jnp.pad(h, ((0, 0), (CONV_WIDTH - 1, 0), (0, 0)))
    y = hp[:, 0:S] * w[0]
    for j in range(1, CONV_WIDTH):
        y = y + hp[:, j:j + S] * w[j]
    return y + b


def setup_inputs(seed: int = 0) -> dict:
    key = jax.random.key(seed)
    ks = jax.random.split(key, 16)
    f32 = jnp.float32
    nrm = lambda k, shape, scale: jax.random.normal(k, shape, f32) * scale
    return {
        "x": nrm(ks[0], (BATCH, SEQ, D_MODEL), 1.0),
        "mix_norm_g": 1.0 + nrm(ks[1], (DEPTH, D_MODEL), 0.01),
        "w_in": nrm(ks[2], (DEPTH, D_MODEL, PROJ_WIDTH), D_MODEL ** -0.5),
        "ret_norm_g": 1.0 + nrm(ks[3], (DEPTH, RET_WIDTH), 0.01),
        "sgu_ln_g": 1.0 + nrm(ks[4], (DEPTH, SGU_GROUPS, SGU_GROUP_DIM), 0.01),
        "sgu_ln_b": nrm(ks[5], (DEPTH, SGU_GROUPS, SGU_GROUP_DIM), 0.01),
        "sgu_w_s": nrm(ks[6], (DEPTH, SGU_GROUPS, SGU_CHUNK, SGU_CHUNK), SGU_CHUNK ** -0.5),
        "sgu_b_s": 1.0 + nrm(ks[7], (DEPTH, SGU_GROUPS, SGU_CHUNK), 0.01),
        "w_out": nrm(ks[8], (DEPTH, MIX_WIDTH, D_MODEL), MIX_WIDTH ** -0.5),
        "ffn_norm_g": 1.0 + nrm(ks[9], (DEPTH, D_MODEL), 0.01),
        "w_up": nrm(ks[10], (DEPTH, D_MODEL, 2 * D_FF), D_MODEL ** -0.5),
        "conv_w": nrm(ks[11], (DEPTH, CONV_WIDTH, 2 * D_FF), CONV_WIDTH ** -0.5),
        "conv_b": nrm(ks[12], (DEPTH, 2 * D_FF), 0.01),
        "w_down": nrm(ks[13], (DEPTH, D_FF, D_MODEL), D_FF ** -0.5),
        "final_norm_g": 1.0 + nrm(ks[14], (D_MODEL,), 0.01),
    }


def reference(x, mix_norm_g, w_in, ret_norm_g, sgu_ln_g, sgu_ln_b, sgu_w_s, sgu_b_s,
              w_out, ffn_norm_g, w_up, conv_w, conv_b, w_down, final_norm_g):
    B, S, _ = x.shape
    half = RET_HEAD_DIM // 2
    inv_freq = jnp.power(ROPE_BASE, -jnp.arange(half, dtype=jnp.float32) / half)
    ang = jnp.arange(S, dtype=jnp.float32)[:, None] * inv_freq[None, :]
    cos, sin = jnp.cos(ang).astype(x.dtype), jnp.sin(ang).astype(x.dtype)
    splits = [RET_WIDTH, 2 * RET_WIDTH, 3 * RET_WIDTH, 4 * RET_WIDTH, 4 * RET_WIDTH + SGU_WIDTH]

    for l in range(DEPTH):
        h = rmsnorm(x, mix_norm_g[l])
        proj = h @ w_in[l]
        q, k, v, g, u, sv = jnp.split(proj, splits, axis=-1)
        q = rotary(q.reshape(B, S, RET_HEADS, RET_HEAD_DIM), cos, sin)
        k = rotary(k.reshape(B, S, RET_HEADS, RET_HEAD_DIM), cos, sin) * (RET_HEAD_DIM ** -0.5)
        v = v.reshape(B, S, RET_HEADS, RET_HEAD_DIM)
        ret = retention_chunkwise(q, k, v)
        ret = rmsnorm(ret, ret_norm_g[l].reshape(RET_HEADS, RET_HEAD_DIM)).reshape(B, S, RET_WIDTH)
        ret = jax.nn.silu(g) * ret
        sgu = spatial_gating_chunked(jax.nn.gelu(u, approximate=False),
                                     jax.nn.gelu(sv, approximate=False),
                                     sgu_ln_g[l], sgu_ln_b[l], sgu_w_s[l], sgu_b_s[l])
        mixed = jnp.concatenate([ret, sgu], axis=-1) @ w_out[l]
        x = x + mixed
        h = rmsnorm(x, ffn_norm_g[l])
        up = causal_depthwise_conv(h @ w_up[l], conv_w[l], conv_b[l])
        a, bgate = jnp.split(up, [D_FF], axis=-1)
        x = x + (jax.nn.silu(a) * bgate) @ w_down[l]

    return rmsnorm(x, final_norm_g)
```

```python
import os
import contextlib
import types
import numpy as np
import concourse.bass as bass
import concourse.mybir as mybir
from concourse.bass_utils import run_bass_kernel_spmd

F32 = mybir.dt.float32
BF16 = mybir.dt.bfloat16
AF = mybir.ActivationFunctionType
ALU = mybir.AluOpType

D = 1024
S_LEN = 2048
NCH = 16
DFF = 2816
NFC = 22
EPS = 1e-6
QUARTERS = [(0, 5), (5, 10), (10, 16), (16, 22)]
SB_BASE = 16512
SB_END = 229376


class Reg:
    __slots__ = ("name", "w", "r")

    def __init__(self, name):
        self.name = name
        self.w = None
        self.r = {}


def _snap(fn):
    if fn.__closure__ is None:
        return fn
    cells = []
    for c in fn.__closure__:
        try:
            cells.append(types.CellType(c.cell_contents))
        except ValueError:
            cells.append(c)
    g = types.FunctionType(fn.__code__, fn.__globals__, fn.__name__, fn.__defaults__, tuple(cells))
    g.__kwdefaults__ = fn.__kwdefaults__
    return g


class Sched:
    ENGS = ("pe", "act", "dve", "pool", "sp")

    def __init__(self, nc):
        self.nc = nc
        self.prog = {e: [] for e in self.ENGS}
        self.cnt = {e: 0 for e in self.ENGS}
        self.known = {e: {} for e in self.ENGS}
        self.dma_sems = []

    def new_dma_sem(self):
        self.dma_sems.append(0)
        return len(self.dma_sems) - 1

    def _need(self, eng, deps, key, val):
        if self.known[eng].get(key, 0) >= val:
            return
        if deps.get(key, 0) < val:
            deps[key] = val

    def _add_dep(self, eng, deps, tok):
        if tok[0] == "dma":
            self._need(eng, deps, ("dma", tok[1]), tok[2])
        else:
            e, s = tok
            if e == eng and e == "pe":
                return
            self._need(eng, deps, e, s)

    def _collect(self, eng, reads, writes):
        deps = {}
        for r in reads:
            if r.w is not None:
                self._add_dep(eng, deps, r.w)
        for w in writes:
            if w.w is not None:
                self._add_dep(eng, deps, w.w)
            for k, v in w.r.items():
                if isinstance(k, tuple):
                    self._add_dep(eng, deps, ("dma", k[1], v))
                else:
                    self._add_dep(eng, deps, (k, v))
        return deps

    def _emit_waits(self, eng, deps):
        for key, val in deps.items():
            self.known[eng][key] = val
            self.prog[eng].append(("wait", key, val))

    def op(self, eng, fn, reads=(), writes=(), inc=True):
        deps = self._collect(eng, reads, writes)
        self._emit_waits(eng, deps)
        if inc:
            self.cnt[eng] += 1
            seq = self.cnt[eng]
        else:
            seq = self.cnt[eng] + 1
        self.prog[eng].append(("op", _snap(fn), inc))
        for r in reads:
            if r.r.get(eng, 0) < seq:
                r.r[eng] = seq
        for w in writes:
            w.w = (eng, seq)
            w.r = {}
        return seq

    def dma(self, eng, fn, sem, reads=(), writes=()):
        deps = self._collect(eng, reads, writes)
        self._emit_waits(eng, deps)
        self.dma_sems[sem] += 16
        cnt = self.dma_sems[sem]
        self.prog[eng].append(("dma", _snap(fn), sem))
        for r in reads:
            r.r[("dma", sem)] = cnt
        for w in writes:
            w.w = ("dma", sem, cnt)
            w.r = {}
        return cnt

    def wait_all_dma(self, eng, sem):
        self.prog[eng].append(("wait", ("dma", sem), self.dma_sems[sem]))

    def build(self):
        nc = self.nc
        with contextlib.ExitStack() as st:
            esem = {e: st.enter_context(nc.semaphore("s_" + e)) for e in self.ENGS}
            dsem = [st.enter_context(nc.semaphore("d_%d" % i)) for i in range(len(self.dma_sems))]
            block = st.enter_context(nc.Block())

            def replay(e, engine):
                for item in self.prog[e]:
                    if item[0] == "wait":
                        key, val = item[1], item[2]
                        if isinstance(key, tuple):
                            engine.wait_ge(dsem[key[1]], val)
                        else:
                            engine.wait_ge(esem[key], val)
                    elif item[0] == "op":
                        ins = item[1](engine)
                        if item[2]:
                            ins.then_inc(esem[e], 1)
                    else:
                        ins = item[1](engine)
                        ins.then_inc(dsem[item[2]], 16)

            @block.tensor
            def _(eng):
                replay("pe", eng)

            @block.scalar
            def _(eng):
                replay("act", eng)

            @block.vector
            def _(eng):
                replay("dve", eng)

            @block.gpsimd
            def _(eng):
                replay("pool", eng)

            @block.sync
            def _(eng):
                replay("sp", eng)


class Arena:
    def __init__(self, nc, start, end, tag):
        self.nc, self.off, self.end, self.tag = nc, start, end, tag
        self.n = 0

    def alloc(self, name, shape, dt):
        nbytes = int(np.prod(shape[1:])) * (4 if dt == F32 else 2)
        nbytes = (nbytes + 31) // 32 * 32
        assert self.off + nbytes <= self.end, (self.tag, name, self.off, nbytes, self.end)
        t = self.nc.alloc_sbuf_tensor_at("%s_%s" % (self.tag, name), list(shape), dt, offset=self.off)
        self.off += nbytes
        return t


class SafeRot:
    def __init__(self, items):
        self.items, self.i = items, 0
        self.open = [False] * len(items)

    def next(self):
        for _ in range(len(self.items)):
            k = self.i % len(self.items)
            self.i += 1
            if not self.open[k]:
                self.open[k] = True
                return self.items[k]
        raise AssertionError("SafeRot exhausted")

    def release(self, item):
        for k, it in enumerate(self.items):
            if it is item or it[1] is item[1]:
                self.open[k] = False
                return
        raise AssertionError("release of unknown item")


class Rot:
    def __init__(self, items):
        self.items, self.i = items, 0

    def next(self):
        it = self.items[self.i % len(self.items)]
        self.i += 1
        return it


def build_program(debug=False):
    nc = bass.Bass("TRN2", target_bir_lowering=False)
    S = Sched(nc)

    def din(name, shape):
        return nc.dram_tensor(name, list(shape), F32, kind="ExternalInput").ap()

    x_d = din("x", [S_LEN, D])
    w_in_d = din("w_in", [D, 3072])
    w_out_d = din("w_out", [D, D])
    w_up_d = din("w_up", [D, 2 * DFF])
    w_down_d = din("w_down", [DFF, D])
    g1c_d = din("g1c", [128, 8])
    g2c_d = din("g2c", [128, 8])
    gf_d = din("gf", [D])
    gret_d = din("gret", [128, 4])
    lng_d = din("lng", [128, 4])
    lnb_d = din("lnb", [128, 4])
    wst_d = din("wst", [128, 4, 128])
    bs_d = din("bs", [512])
    cw_d = din("cw", [128, 44, 3])
    cb_d = din("cb", [128, 44])
    rotc_d = din("rotc", [128, NCH, 128])
    rots_d = din("rots", [128, NCH, 128])
    ckq_d = din("ckq", [128, NCH, 4, 128])
    skq_d = din("skq", [128, NCH, 4, 128])
    mask_d = din("mask", [128, 4, 128])
    epsr_d = din("epsr", [128, 4, 128])
    tril_d = din("tril", [128, 4, 128])
    ident_d = din("ident", [128, 128])
    out_d = nc.dram_tensor("out", [S_LEN, D], F32, kind="ExternalOutput").ap()
    dbg_d = None
    if debug:
        dbg_d = nc.dram_tensor("dbg", [S_LEN, D], F32, kind="ExternalOutput").ap()

    top = Arena(nc, SB_BASE, SB_END, "t")
    X1 = top.alloc("X1", [128, NCH, D], F32)
    H2T = top.alloc("H2T", [128, 8, S_LEN], BF16)
    identb = top.alloc("identb", [128, 128], BF16)
    onesb = top.alloc("onesb", [128, 128], BF16)
    g1c = top.alloc("g1c", [128, 8], F32)
    g2c = top.alloc("g2c", [128, 8], F32)
    cw = top.alloc("cw", [128, 44, 3], F32)
    cb = top.alloc("cb", [128, 44], F32)
    ss1 = top.alloc("ss1", [128, NCH], F32)
    rs1 = top.alloc("rs1", [128, NCH], F32)
    ss2 = top.alloc("ss2", [128, NCH], F32)
    rs2 = top.alloc("rs2", [128, NCH], F32)
    ssf = top.alloc("ssf", [128, NCH], F32)
    rsf = top.alloc("rsf", [128, NCH], F32)
    epsb = top.alloc("epsb", [128, 2], F32)
    bar_s = top.alloc("bar_s", [128, 8], F32)
    R_START = top.off
    a1 = Arena(nc, R_START, SB_END, "m1")
    a2 = Arena(nc, R_START, SB_END, "m2")

    hT = a1.alloc("hT", [128, 8, 512], BF16)
    rA = Rot([(a1.alloc("rA%d" % i, [128, 4, 128], F32), Reg("rA%d" % i)) for i in range(1)])
    rB = Rot([(a1.alloc("rB%d" % i, [128, 4, 128], F32), Reg("rB%d" % i)) for i in range(1)])
    ktab = Rot([((a1.alloc("ckt%d" % i, [128, 4, 128], F32), a1.alloc("skt%d" % i, [128, 4, 128], F32)), (Reg("ckt%d" % i), Reg("skt%d" % i))) for i in range(2)])
    rotc = a1.alloc("rotc", [128, 4, 128], F32)
    rots = a1.alloc("rots", [128, 4, 128], F32)
    ring1 = [(a1.alloc("ring%d" % i, [128, 8, 512], BF16), Reg("ring1_%d" % i)) for i in range(2)]
    identf = a1.alloc("identf", [128, 128], F32)
    maskt = a1.alloc("mask", [128, 4, 128], F32)
    epsbf = a1.alloc("epsbf", [128, 512], BF16)
    wtm = a1.alloc("wtm", [128, 4, 128], BF16)
    bias2 = a1.alloc("bias2", [128, 4, 128], F32)
    gret = a1.alloc("gret", [128, 4], F32)
    lng = a1.alloc("lng", [128, 4], F32)
    lnb = a1.alloc("lnb", [128, 4], F32)
    hn = Rot([(a1.alloc("hn%d" % i, [128, D], BF16), Reg("hn%d" % i)) for i in range(2)])
    hT_r = [Reg("hT_%d" % j) for j in range(4)]
    qtm = Rot([(a1.alloc("qtm%d" % i, [128, 4, 128], BF16), Reg("qtm%d" % i)) for i in range(2)])
    ktm = Rot([(a1.alloc("ktm%d" % i, [128, 4, 128], BF16), Reg("ktm%d" % i)) for i in range(4)])
    vtm = Rot([(a1.alloc("vtm%d" % i, [128, 4, 128], BF16), Reg("vtm%d" % i)) for i in range(4)])
    ztm = Rot([(a1.alloc("ztm%d" % i, [128, 4, 128], BF16), Reg("ztm%d" % i)) for i in range(4)])
    qT = Rot([(a1.alloc("qT%d" % i, [128, 4, 128], BF16), Reg("qT%d" % i)) for i in range(4)])
    kT = Rot([(a1.alloc("kT%d" % i, [128, 4, 128], BF16), Reg("kT%d" % i)) for i in range(4)])
    sg = a1.alloc("sg", [128, 4, 512], BF16)
    gu = a1.alloc("gu", [128, 4, 512], BF16)
    sg_r = [Reg("sg%d" % h) for h in range(4)]
    gu_r = [Reg("gu%d" % h) for h in range(4)]
    st_sgu = a1.alloc("st_sgu", [128, 7, 16], F32)
    smb = Rot([(a1.alloc("sm%d" % i, [128, 4, 128], BF16), Reg("sm%d" % i)) for i in range(2)])
    Sst = a1.alloc("Sst", [128, 4, 128], F32)
    Sbf = Rot([(a1.alloc("Sbf%d" % i, [128, 4, 128], BF16), Reg("Sbf%d" % i)) for i in range(2)])
    sqb2 = Rot([(a1.alloc("sqc%d" % i, [128, 512], BF16), Reg("sqc%d" % i)) for i in range(2)])
    rsb = a1.alloc("rsb", [128, 512], F32)
    o1 = a1.alloc("o1", [128, 4, 128], F32)
    mx = a1.alloc("mx", [128, 4, 128], F32)
    rsbB = Rot([(rsb, Reg("rsb")), (a1.alloc("rsb2", [128, 512], F32), Reg("rsb2"))])
    mxB = Rot([(mx, None), (a1.alloc("mxb", [128, 4, 128], F32), Reg("mxb"))])
    th = o1[:, :, :].rearrange("p h q -> p (h q)")
    gvb = Rot([(a1.alloc("gv%d" % i, [128, 4, 128], BF16), Reg("gv%d" % i)) for i in range(4)])
    catb = Rot([(a1.alloc("cat%d" % i, [128, 8, 128], BF16), Reg("cat%d" % i)) for i in range(2)])

    ring2 = [(a2.alloc("ring%d" % i, [128, 6144], BF16), Reg("ring2_%d" % i)) for i in range(6)]
    gfb = a2.alloc("gfb", [128, D], F32)
    halo = a2.alloc("halo", [128, 12, 2, 2], F32)
    Ub = Rot([(a2.alloc("U%d" % i, [128, 514], F32), Reg("U%d" % i)) for i in range(3)])
    Ya = Rot([(a2.alloc("Ya%d" % i, [128, 512], F32), Reg("Ya%d" % i)) for i in range(2)])
    Yb = Rot([(a2.alloc("Yb%d" % i, [128, 512], F32), Reg("Yb%d" % i)) for i in range(2)])
    Sa = Rot([(a2.alloc("Sa%d" % i, [128, 512], F32), Reg("Sa%d" % i)) for i in range(1)])
    actb = Rot([(a2.alloc("act%d" % i, [128, 6, 512], BF16), [Reg("act%d_%d" % (i, f)) for f in range(6)]) for i in range(2)])
    ostg = Rot([(a2.alloc("ostg%d" % i, [128, D], F32), Reg("ostg%d" % i)) for i in range(1)])

    pbank = Rot([(nc.alloc_psum_tensor("pb%d" % i, [128, 512], F32), Reg("pb%d" % i)) for i in range(6)])
    ptb = Rot([(nc.alloc_psum_tensor("pt%d" % i, [128, 1024], BF16), Reg("pt%d" % i)) for i in range(2)])
    pbank_all = pbank
    pbank4 = SafeRot(pbank.items[0:4])
    pbo2 = Rot(pbank.items[4:6])

    RG = {}

    def R(name):
        if name not in RG:
            RG[name] = Reg(name)
        return RG[name]

    x1_r = [Reg("x1_%d" % n) for n in range(NCH)]
    h2t_r = [Reg("h2t_%d" % n) for n in range(NCH)]
    mp1_all = Reg("mp1_all")

    sem_c = S.new_dma_sem()
    sem_x = [S.new_dma_sem() for _ in range(NCH)]
    sem_rot = S.new_dma_sem()
    sem_rot2 = S.new_dma_sem()
    sem_ring1 = [S.new_dma_sem() for _ in range(2)]
    sem_ring2 = [S.new_dma_sem() for _ in range(6)]
    sem_out = S.new_dma_sem()
    sem_c2 = S.new_dma_sem()
    ktab_sems = [(S.new_dma_sem(), S.new_dma_sem()) for _ in range(2)]

    def ld(eng, dst, src, reg, sem=sem_c, extra_w=()):
        S.dma(eng, lambda e: e.dma_start(out=dst, in_=src), sem, writes=[reg] + list(extra_w))

    sem_c0 = S.new_dma_sem()
    ld("act", identf[:, :], ident_d[:, :], R("identf"), sem=sem_c0)
    ld("act", g1c[:, :], g1c_d[:, :], R("g1c"), sem=sem_c0)
    for _nm in ("identf", "g1c"):
        RG[_nm].w = ("dma", sem_c0, S.dma_sems[sem_c0])
    def load_x_block(b, gate=()):
        for j in range(4):
            n = 4 * b + j
            S.dma("sp", lambda e: e.dma_start(out=X1[:, n, :], in_=x_d[n * 128:(n + 1) * 128, :]), sem_x[n],
                  reads=list(gate), writes=[x1_r[n]])

    for b in range(1):
        load_x_block(0)
        if b == 0:
            ld("act", g2c[:, :], g2c_d[:, :], R("g2c"))
            ld("act", maskt[:, :, :], mask_d[:, :, :], R("mask"))
            ld("act", gret[:, :], gret_d[:, :], R("gret"))
            ld("act", lng[:, :], lng_d[:, :], R("lng"))
            ld("act", lnb[:, :], lnb_d[:, :], R("lnb"))
            ld("act", cw[:, :, :], cw_d[:, :, :], R("cw"))
            ld("act", cb[:, :], cb_d[:, :], R("cb"))
            ld("act", o1[:, :, :], wst_d[:, :, :], R("o1"))
            ld("act", mx[:, :, :], tril_d[:, :, :], R("mx"))
            ld("act", rsb[:, :], bs_d.partition_broadcast(128), R("rsb"))

    for _nm in ("g2c", "mask", "gret", "lng", "lnb", "cw", "cb", "o1", "mx", "rsb"):
        RG[_nm].w = ("dma", sem_c, S.dma_sems[sem_c])
    S.op("dve", lambda e: e.tensor_copy(out=identb[:, :], in_=identf[:, :]), reads=[R("identf")], writes=[R("identb")])
    S.op("dve", lambda e: e.memset(onesb[:, :], 1.0), writes=[R("onesb")])
    S.op("dve", lambda e: e.memset(Sst[:, :, :], 0.0), writes=[R("Sst%d" % h) for h in range(4)])
    sbf_t, sbf_r = Sbf.next()
    S.op("dve", lambda e, t=sbf_t: e.memset(t[:, :, :], 0.0), writes=[sbf_r])
    cur_sbf = (sbf_t, sbf_r)

    slab_issued = [0]

    def slab_src(i):
        c = i % 8
        if c < 6:
            return w_in_slab(SLAB_ORDER[c])
        return w_out_slab(c - 6)

    def slab_issue_upto(i):
        while slab_issued[0] <= min(i, 31):
            k = slab_issued[0]
            t, r = ring1[k % 2]
            src = slab_src(k)
            S.dma("pool", lambda e: e.dma_start(out=t[:, :, :], in_=src), sem_ring1[k % 2], writes=[r])
            slab_issued[0] += 1

    def slab_get(i):
        slab_issue_upto(i)
        return ring1[i % 2]

    def slab_consumed(i):
        slab_issue_upto(i + 2)


    def w_in_slab(c):
        return w_in_d[:, c * 512:(c + 1) * 512].rearrange("(k p) c -> p k c", p=128)

    def w_out_slab(c):
        return w_out_d[:, c * 512:(c + 1) * 512].rearrange("(k p) c -> p k c", p=128)

    SLAB_ORDER = [0, 5, 1, 2, 3, 4]

    def norm_to_T(n, j, src_stat, rstat, gcol, greg, dstT, dst_col0, dst_reg, stat_reg):
        ht, hr = hn.next()
        S.op("act", lambda e: e.activation(out=ht[:, :], in_=X1[:, n, :], func=AF.Identity, scale=rstat[:, n:n + 1]),
             reads=[x1_r[n], stat_reg], writes=[hr])
        pt, ptr = ptb.next()
        for k in range(8):
            S.op("pe", lambda e, k=k: e.transpose(out=pt[:, k * 128:(k + 1) * 128], in_=ht[:, k * 128:(k + 1) * 128], identity=identb[:, :]),
                 reads=[hr, R("identb")], writes=[ptr], inc=(k == 7))
        S.op("dve", lambda e: e.tensor_tensor(out=dstT[:, :, dst_col0:dst_col0 + 128], in0=pt[:, :].rearrange("p (k t) -> p k t", k=8),
                                              in1=gcol[:, :].rearrange("p (k o) -> p k o", o=1).to_broadcast([128, 8, 128]), op=ALU.mult),
             reads=[ptr, greg], writes=[dst_reg])

    def stats(nlist, sstat, rstat, stat_reg, junk, junk_reg, xregs):
        for n in nlist:
            if junk is None:
                jt, jr = hn.next()
            else:
                jt, jr = junk, junk_reg
            S.op("act", lambda e, n=n, jt=jt: e.activation(out=jt[:, :], in_=X1[:, n, :], func=AF.Square, accum_out=sstat[:, n:n + 1]),
                 reads=[xregs[n]], writes=[jr, stat_reg])
        n0, n1 = nlist[0], nlist[-1] + 1
        S.op("act", lambda e: e.activation(out=rstat[:, n0:n1], in_=sstat[:, n0:n1], func=AF.Ln, scale=1.0 / D, bias=epsb[:, 0:1]),
             reads=[stat_reg, R("epsb")], writes=[stat_reg])
        S.op("act", lambda e: e.activation(out=rstat[:, n0:n1], in_=rstat[:, n0:n1], func=AF.Exp, scale=-0.5),
             reads=[stat_reg], writes=[stat_reg])

    S.op("dve", lambda e: e.memset(epsb[:, 0:1], EPS), writes=[R("epsb")])
    S.op("dve", lambda e: e.memset(epsb[:, 1:2], 0.0), writes=[R("epsb")])

    dbg_taps = []

    sem_eps = S.new_dma_sem()
    S.dma("pool", lambda e: e.dma_start(out=epsbf[:, :], in_=epsr_d.rearrange("p h q -> p (h q)")), sem_eps, writes=[R("epsbf")])

    def setup_late():
        S.op("dve", lambda e: e.tensor_scalar(out=gret[:, :], in0=gret[:, :], scalar1=0.5, scalar2=None, op0=ALU.mult),
             reads=[R("gret")], writes=[R("gret")])
        S.op("dve", lambda e: e.tensor_tensor(out=wtm[:, :, :], in0=o1[:, :, :], in1=mx[:, :, :], op=ALU.mult),
             reads=[R("o1"), R("mx")], writes=[R("wtm")])
        pb, pr = pbank.next()
        S.op("pe", lambda e: e.matmul(pb[:, :], lhsT=onesb[:, :], rhs=wtm[:, :, :].rearrange("p g t -> p (g t)"), start=True, stop=True),
             reads=[R("onesb"), R("wtm")], writes=[pr])
        for g in range(4):
            S.op("dve", lambda e, g=g, pb=pb: e.scalar_tensor_tensor(out=bias2[:, g, :], in0=pb[:, g * 128:(g + 1) * 128], scalar=lnb[:, g:g + 1],
                                                                     in1=rsb[:, g * 128:(g + 1) * 128], op0=ALU.mult, op1=ALU.add),
                 reads=[pr, R("lnb"), R("rsb")], writes=[R("bias2")])

    ring2_i = [0]

    def load_quarter_ab(q, guardA=(), guardB=()):
        f0, f1 = QUARTERS[q]
        nf = f1 - f0
        wa, war = ring2[ring2_i[0] % 6]
        i = ring2_i[0] % 6
        ring2_i[0] += 1
        S.dma("pool", lambda e: e.dma_start(out=wa[:, 0:8 * nf * 128].rearrange("p (k c) -> p k c", k=8),
                                            in_=w_up_d[:, f0 * 128:f1 * 128].rearrange("(k p) c -> p k c", p=128)),
              sem_ring2[i], writes=[war] + list(guardA))
        wb_, wbr = ring2[ring2_i[0] % 6]
        i2 = ring2_i[0] % 6
        ring2_i[0] += 1
        S.dma("pool", lambda e: e.dma_start(out=wb_[:, 0:8 * nf * 128].rearrange("p (k c) -> p k c", k=8),
                                            in_=w_up_d[:, DFF + f0 * 128:DFF + f1 * 128].rearrange("(k p) c -> p k c", p=128)),
              sem_ring2[i2], writes=[wbr] + list(guardB))
        return (wa, war), (wb_, wbr), nf

    def load_quarter_d(q, ab):
        (wa, war), (wb_, wbr), nf = ab
        f0, f1 = QUARTERS[q]
        wd, wdr = ring2[ring2_i[0] % 6]
        i3 = ring2_i[0] % 6
        ring2_i[0] += 1
        S.dma("pool", lambda e: e.dma_start(out=wd[:, 0:nf * 1024].rearrange("p (f c) -> p f c", f=nf),
                                            in_=w_down_d[f0 * 128:f1 * 128, :].rearrange("(f p) c -> p f c", p=128)),
              sem_ring2[i3], writes=[wdr])
        return (wa, war), (wb_, wbr), (wd, wdr), nf

    q0_ab = [None]

    deferred = []

    def defer(n, thunk):
        deferred.append([n, thunk])

    def pe_tick():
        for it in deferred:
            it[0] -= 1
        for it in list(deferred):
            if it[0] <= 0:
                deferred.remove(it)
                it[1]()

    def flush_deferred():
        while deferred:
            deferred.pop(0)[1]()

    def phase_A(b):
        nl = [4 * b + j for j in range(4)]
        for j in range(4):
            norm_to_T(nl[j], j, ss1, rs1, g1c, R("g1c"), hT, j * 128, hT_r[j], R("st1"))

    slab_issue_upto(1)
    stats([0, 1, 2, 3], ss1, rs1, R("st1"), None, None, x1_r)
    phase_A(0)
    setup_late()
    for b in range(4):
        nl = [4 * b + j for j in range(4)]
        S.dma("sp", lambda e: e.dma_start(out=rotc[:, :, :], in_=rotc_d[:, 4 * b:4 * b + 4, :]), sem_rot, writes=[R("rotc")])
        S.dma("sp", lambda e: e.dma_start(out=rots[:, :, :], in_=rots_d[:, 4 * b:4 * b + 4, :]), sem_rot2, writes=[R("rots")])

        if b + 1 < 4:
            load_x_block(b + 1, gate=[hT_r[3]])
            stats([4 * (b + 1) + j for j in range(4)], ss1, rs1, R("st1"), None, None, x1_r)
        chunk = [dict() for _ in range(4)]
        for ci, c in enumerate(SLAB_ORDER[:4]):
            wt, wr = slab_get(8 * b + ci)
            for j in range(4):
                pb, pr = pbank.next()
                for k in range(8):
                    S.op("pe", lambda e, k=k: e.matmul(pb[:, :], lhsT=hT[:, k, j * 128:(j + 1) * 128], rhs=wt[:, k, :],
                                                       start=(k == 0), stop=(k == 7)),
                         reads=[hT_r[j], wr], writes=[pr], inc=(k == 7))
                pv = pb[:, :].rearrange("p (h d) -> p h d", h=4)
                n = nl[j]
                if c in (0, 1):
                    if c == 0:
                        ctab = rotc[:, j:j + 1, :].to_broadcast([128, 4, 128])
                        stab = rots[:, j:j + 1, :].rearrange("p o (two d) -> p o two d", two=2).to_broadcast([128, 4, 2, 64])
                        tabs = [R("rotc"), R("rots")]
                    else:
                        (ckt, skt), (ckr, skr) = ktab.next()
                        ksem_c, ksem_s = ktab_sems[(ktab.i - 1) % 2]
                        S.dma("sp", lambda e: e.dma_start(out=ckt[:, :, :], in_=ckq_d[:, n, :, :]), ksem_c, writes=[ckr])
                        S.dma("sp", lambda e: e.dma_start(out=skt[:, :, :], in_=skq_d[:, n, :, :]), ksem_s, writes=[skr])
                        ctab = ckt[:, :, :]
                        stab = skt[:, :, :].rearrange("p h (two d) -> p h two d", two=2)
                        tabs = [ckr, skr]
                    At, Ar = rA.next()
                    Bt, Br = rB.next()
                    S.op("dve", lambda e: e.tensor_tensor(out=At[:, :, :], in0=pv, in1=ctab, op=ALU.mult),
                         reads=[pr] + tabs, writes=[Ar])
                    pvsw = pb[:, :].rearrange("p (h two d) -> p h two d", h=4, two=2)[:, :, ::-1, :]
                    S.op("dve", lambda e: e.tensor_tensor(out=Bt[:, :, :].rearrange("p h (two d) -> p h two d", two=2), in0=pvsw, in1=stab, op=ALU.mult),
                         reads=[pr] + tabs, writes=[Br])
                    dt_, dr_ = (qtm.next() if c == 0 else ktm.next())
                    S.op("dve", lambda e: e.tensor_tensor(out=dt_[:, :, :], in0=At[:, :, :], in1=Bt[:, :, :], op=ALU.add),
                         reads=[Ar, Br], writes=[dr_])
                    Tt, Tr = (qT.next() if c == 0 else kT.next())

                    def do_T(dt_=dt_, dr_=dr_, Tt=Tt, Tr=Tr):
                        pt, ptr = ptb.next()
                        for h in range(4):
                            S.op("pe", lambda e, h=h: e.transpose(out=pt[:, h * 128:(h + 1) * 128], in_=dt_[:, h, :], identity=identb[:, :]),
                                 reads=[dr_, R("identb")], writes=[ptr], inc=(h == 3))
                        S.op("act", lambda e: e.activation(out=Tt[:, :, :].rearrange("p h t -> p (h t)"), in_=pt[:, 0:512], func=AF.Copy),
                             reads=[ptr], writes=[Tr])
                    defer(2, do_T)
                    chunk[j]["q" if c == 0 else "k"] = (dt_, dr_)
                    chunk[j]["qT" if c == 0 else "kT"] = (Tt, Tr)
                elif c == 2:
                    vt, vr = vtm.next()
                    S.op("act", lambda e: e.activation(out=vt[:, :, :].rearrange("p h d -> p (h d)"), in_=pb[:, :], func=AF.Copy),
                         reads=[pr], writes=[vr])
                    chunk[j]["v"] = (vt, vr)
                else:
                    gvt, gvr = gvb.next()
                    for g in range(4):
                        S.op("act", lambda e, g=g: e.activation(out=gvt[:, g, :], in_=pv[:, g, :], func=AF.Gelu,
                                                                accum_out=st_sgu[:, 0, j * 4 + g:j * 4 + g + 1]),
                             reads=[pr], writes=[gvr, R("st_s1")])
                    for g in range(4):
                        sqj, sqjr = sqb2.items[0]
                        S.op("dve", lambda e, g=g: e.scalar_tensor_tensor(out=sqj[:, 0:128], in0=gvt[:, g, :], scalar=1.0, in1=gvt[:, g, :],
                                                                          op0=ALU.mult, op1=ALU.mult,
                                                                          accum_out=st_sgu[:, 1, j * 4 + g:j * 4 + g + 1]),
                             reads=[gvr], writes=[sqjr, R("st_s2")])
                    chunk[j]["gv"] = (gvt, gvr)
                pe_tick()
            slab_consumed(8 * b + ci)
        S.op("dve", lambda e: e.tensor_scalar(out=st_sgu[:, 2, :], in0=st_sgu[:, 0, :], scalar1=1.0 / 128, scalar2=None, op0=ALU.mult),
             reads=[R("st_sgu"), R("st_s1"), R("st_s2")], writes=[R("st_sgu")])
        S.op("dve", lambda e: e.tensor_tensor(out=st_sgu[:, 3, :], in0=st_sgu[:, 2, :], in1=st_sgu[:, 2, :], op=ALU.mult),
             reads=[R("st_sgu")], writes=[R("st_sgu")])
        S.op("dve", lambda e: e.scalar_tensor_tensor(out=st_sgu[:, 4, :], in0=st_sgu[:, 1, :], scalar=1.0 / 128, in1=st_sgu[:, 3, :],
                                                     op0=ALU.mult, op1=ALU.subtract),
             reads=[R("st_sgu"), R("st_s2")], writes=[R("st_sgu")])
        S.op("act", lambda e: e.activation(out=st_sgu[:, 5, :], in_=st_sgu[:, 4, :], func=AF.Ln, bias=epsb[:, 0:1]),
             reads=[R("st_sgu"), R("epsb")], writes=[R("st_sgu")])
        S.op("act", lambda e: e.activation(out=st_sgu[:, 5, :], in_=st_sgu[:, 5, :], func=AF.Exp, scale=-0.5),
             reads=[R("st_sgu")], writes=[R("st_sgu")])
        S.op("dve", lambda e: e.scalar_tensor_tensor(out=st_sgu[:, 6, :], in0=st_sgu[:, 2, :], scalar=-1.0, in1=st_sgu[:, 5, :],
                                                     op0=ALU.mult, op1=ALU.mult),
             reads=[R("st_sgu")], writes=[R("st_sgu")])
        for j in range(4):
            gvt, gvr = chunk[j]["gv"]
            zt, zr = ztm.next()
            for g in range(4):
                S.op("dve", lambda e, g=g: e.tensor_scalar(out=zt[:, g, :], in0=gvt[:, g, :], scalar1=st_sgu[:, 5, j * 4 + g:j * 4 + g + 1],
                                                           scalar2=st_sgu[:, 6, j * 4 + g:j * 4 + g + 1], op0=ALU.mult, op1=ALU.add),
                     reads=[gvr, R("st_sgu")], writes=[zr])
            chunk[j]["z"] = (zt, zr)

        for c in (3, 4):
            wt, wr = slab_get(8 * b + c + 1)
            for h in range(4):
                pb, pr = pbank.next()
                for k in range(8):
                    S.op("pe", lambda e, k=k: e.matmul(pb[:, :], lhsT=wt[:, k, h * 128:(h + 1) * 128],
                                                       rhs=hT[:, k, :], start=(k == 0), stop=(k == 7)),
                         reads=hT_r + [wr], writes=[pr], inc=(k == 7))
                if c == 3:
                    S.op("act", lambda e: e.activation(out=th, in_=pb[:, :], func=AF.Tanh, scale=0.5), reads=[pr], writes=[R("o1")])
                    S.op("dve", lambda e: e.scalar_tensor_tensor(out=th, in0=th, scalar=1.0, in1=pb[:, :], op0=ALU.add, op1=ALU.mult),
                         reads=[R("o1"), pr], writes=[R("o1")])
                    S.op("act", lambda e: e.activation(out=sg[:, h, :], in_=th, func=AF.Identity, scale=gret[:, h:h + 1]),
                         reads=[R("o1"), R("gret")], writes=[sg_r[h]])
                else:
                    S.op("act", lambda e: e.activation(out=gu[:, h, :], in_=pb[:, :], func=AF.Gelu), reads=[pr], writes=[gu_r[h]])
                pe_tick()
            slab_consumed(8 * b + c + 1)
        flush_deferred()
        if b == 3:
            q0_ab[0] = load_quarter_ab(0, guardA=hT_r + [r for _, r in rA.items] + [r for _, r in rB.items],
                                       guardB=[r for _, rr in ktab.items for r in rr] + [R("rotc"), R("rots")])

        wo = [slab_get(8 * b + 6), slab_get(8 * b + 7)]

        if b + 1 < 4:
            phase_A(b + 1)

        mixst = [dict() for _ in range(4)]

        def H1a(j):
            nonlocal cur_sbf
            cj = chunk[j]
            qTt, qTr = cj["qT"]
            kTt, kTr = cj["kT"]
            kt, kr = cj["k"]
            vt, vr = cj["v"]
            catT, catr = catb.next()
            mixst[j]["cat"] = (catT, catr)
            pbs, prs = pbank4.next()
            for h in range(4):
                S.op("pe", lambda e, h=h: e.matmul(pbs[:, h * 128:(h + 1) * 128], lhsT=kTt[:, h, :], rhs=qTt[:, h, :], start=True, stop=True),
                     reads=[kTr, qTr], writes=[prs], inc=(h == 3))
            pbk, prk = pbank4.next()
            for h in range(4):
                S.op("pe", lambda e, h=h: e.matmul(pbk[:, h * 128:(h + 1) * 128], lhsT=kt[:, h, :], rhs=vt[:, h, :], start=True, stop=True),
                     reads=[kr, vr], writes=[prk], inc=(h == 3))
            smt, smr = smb.next()
            S.op("dve", lambda e: e.tensor_tensor(out=smt[:, :, :].rearrange("p h q -> p (h q)"), in0=pbs[:, :],
                                                  in1=maskt[:, :, :].rearrange("p h q -> p (h q)"), op=ALU.mult),
                 reads=[prs, R("mask")], writes=[smr])
            pbank4.release((pbs, prs))
            pbo, pro = pbo2.next()
            sbt, sbr = cur_sbf
            for h in range(4):
                S.op("pe", lambda e, h=h: e.matmul(pbo[:, h * 128:(h + 1) * 128], lhsT=vt[:, h, :], rhs=smt[:, h, :], start=True, stop=False),
                     reads=[vr, smr], writes=[pro], inc=False)
                S.op("pe", lambda e, h=h: e.matmul(pbo[:, h * 128:(h + 1) * 128], lhsT=sbt[:, h, :], rhs=qTt[:, h, :], start=False, stop=True),
                     reads=[sbr, qTr], writes=[pro], inc=(h == 3))
            for h in range(4):
                cd = float((1.0 - 2.0 ** (-5 - h)) ** 128)
                S.op("dve", lambda e, h=h, cd=cd: e.scalar_tensor_tensor(out=Sst[:, h, :], in0=Sst[:, h, :], scalar=cd,
                                                                        in1=pbk[:, h * 128:(h + 1) * 128], op0=ALU.mult, op1=ALU.add),
                     reads=[R("Sst%d" % h), prk], writes=[R("Sst%d" % h)])
            pbank4.release((pbk, prk))
            nsb_t, nsb_r = Sbf.next()
            S.op("act", lambda e: e.activation(out=nsb_t[:, :, :], in_=Sst[:, :, :], func=AF.Copy),
                 reads=[R("Sst%d" % h) for h in range(4)], writes=[nsb_r])
            cur_sbf = (nsb_t, nsb_r)
            mixst[j]["h1"] = (pbo, pro, catT, catr)

        def H1b(j):
            pbo, pro, catT, catr = mixst[j]["h1"]
            sqt, sqr = sqb2.next()
            rst, rsr = rsbB.next()
            S.op("act", lambda e: e.activation(out=sqt[:, :], in_=pbo[:, :], func=AF.Square), reads=[pro], writes=[sqr])
            yield
            pbq, prq = pbank4.next()
            S.op("pe", lambda e: e.matmul(pbq[:, :], lhsT=onesb[:, :], rhs=sqt[:, :], start=True, stop=False),
                 reads=[R("onesb"), sqr], writes=[prq], inc=False)
            S.op("pe", lambda e: e.matmul(pbq[:, :], lhsT=onesb[0:1, :], rhs=epsbf[0:1, :], start=False, stop=True),
                 reads=[R("onesb"), R("epsbf")], writes=[prq])
            yield
            S.op("act", lambda e: e.activation(out=rst[:, :], in_=pbq[:, :], func=AF.Ln, scale=1.0 / 128), reads=[prq], writes=[rsr])
            pbank4.release((pbq, prq))
            S.op("act", lambda e: e.activation(out=rst[:, :], in_=rst[:, :], func=AF.Exp, scale=-0.5), reads=[rsr], writes=[rsr])
            yield
            S.op("pool", lambda e: e.tensor_tensor(out=rst[:, :].rearrange("p (h q) -> p h q", h=4), in0=rst[:, :].rearrange("p (h q) -> p h q", h=4),
                                                   in1=sg[:, :, j * 128:(j + 1) * 128], op=ALU.mult),
                 reads=[rsr] + sg_r, writes=[rsr])
            yield
            S.op("dve", lambda e: e.tensor_tensor(out=catT[:, 0:4, :].rearrange("p h q -> p (h q)"), in0=pbo[:, :], in1=rst[:, :], op=ALU.mult),
                 reads=[pro, rsr], writes=[catr])
            yield

        def H2(j):
            n = nl[j]
            zt, zr = chunk[j]["z"]
            catT, catr = mixst[j]["cat"]
            mxt, mxr = mxB.next()
            if mxr is None:
                mxr = R("mx")
            pbp, prp = pbank4.next()
            for g in range(4):
                S.op("pe", lambda e, g=g: e.matmul(pbp[:, g * 128:(g + 1) * 128], lhsT=zt[:, g, :], rhs=wtm[:, g, :], start=True, stop=True),
                     reads=[zr, R("wtm")], writes=[prp], inc=(g == 3))
            yield
            for g in range(4):
                S.op("dve", lambda e, g=g: e.scalar_tensor_tensor(out=mxt[:, g, :], in0=pbp[:, g * 128:(g + 1) * 128], scalar=lng[:, g:g + 1],
                                                                 in1=bias2[:, g, :], op0=ALU.mult, op1=ALU.add),
                     reads=[prp, R("lng"), R("bias2")], writes=[mxr])
            pbank4.release((pbp, prp))
            yield
            S.op("pool", lambda e: e.tensor_tensor(out=catT[:, 4:8, :], in0=mxt[:, :, :], in1=gu[:, :, j * 128:(j + 1) * 128], op=ALU.mult),
                 reads=[mxr] + gu_r, writes=[catr])
            yield
            for c in range(2):
                wt, wr = wo[c]
                pb, pr = pbank4.next()
                for f in range(8):
                    S.op("pe", lambda e, f=f: e.matmul(pb[:, :], lhsT=catT[:, f, :], rhs=wt[:, f, :], start=(f == 0), stop=(f == 7)),
                         reads=[catr, wr], writes=[pr], inc=(f == 7))
                S.op("dve", lambda e: e.tensor_tensor(out=X1[:, n, c * 512:(c + 1) * 512], in0=X1[:, n, c * 512:(c + 1) * 512],
                                                      in1=pb[:, :], op=ALU.add),
                     reads=[x1_r[n], pr], writes=[x1_r[n]])
                pbank4.release((pb, pr))
                yield

        def interleave(gens):
            gens = list(gens)
            while gens:
                for g_ in list(gens):
                    try:
                        next(g_)
                    except StopIteration:
                        gens.remove(g_)

        H1a(0)
        H1a(1)
        interleave([H1b(0), H1b(1)])
        H1a(2)
        H1a(3)
        interleave([H2(0), H2(1), H1b(2), H1b(3)])
        interleave([H2(2), H2(3)])
        slab_consumed(8 * b + 6)
        slab_consumed(8 * b + 7)
        stats(nl, ss2, rs2, R("st2"), None, None, x1_r)
        for j in range(4):
            def do_E(j=j, nl=nl):
                n = nl[j]
                norm_to_T(n, j, ss2, rs2, g2c, R("g2c"), H2T, n * 128, h2t_r[n], R("st2"))
            if b < 3 and j >= 2:
                defer(2 * j - 2, do_E)
            else:
                do_E()

    mp1_regs = ([r for _, r in ring1] + [r for _, r in hn.items] + hT_r + [r for _, r in rA.items] + [r for _, r in rB.items]
                + [r for _, r in qtm.items] + [r for _, r in ktm.items] + [r for _, r in vtm.items] + [r for _, r in ztm.items]
                + [r for _, r in qT.items] + [r for _, r in kT.items] + sg_r + gu_r + [r for _, r in smb.items] + [r for _, r in Sbf.items]
                + [r for _, r in catb.items] + [r for _, r in gvb.items] + [r for _, r in sqb2.items] + [r for _, r in rsbB.items] + [r for _, r in mxB.items[1:]] + [r for _, rr in ktab.items for r in rr] + [RG[k] for k in list(RG.keys()) if k not in ("identb", "onesb", "g1c", "g2c", "cw", "cb", "st1", "st2", "epsb")])
    first_mp2 = [True]

    def mp2_guard():
        if first_mp2[0]:
            first_mp2[0] = False
            return mp1_regs
        return []

    S.op("dve", lambda e: e.memset(bar_s[:, 0:1], 0.0), writes=[R("bar_d")] + mp1_regs)
    S.op("act", lambda e: e.activation(out=bar_s[:, 1:2], in_=epsb[:, 0:1], func=AF.Copy), reads=[R("epsb")], writes=[R("bar_a")] + mp1_regs)
    S.op("pool", lambda e: e.memset(bar_s[:, 2:3], 0.0), writes=[R("bar_p")] + mp1_regs)
    S.dma("sp", lambda e: e.dma_start(out=gfb[:, :], in_=gf_d.partition_broadcast(128)), sem_c2, writes=[R("gfb")] + mp1_regs)


    def load_slab2(src_ap, ncols):
        i = ring2_i[0] % 6
        ring2_i[0] += 1
        t, r = ring2[i]
        S.dma("pool", lambda e: e.dma_start(out=t[:, 0:ncols], in_=src_ap), sem_ring2[i], writes=[r] + mp1_regs)
        return t, r

    qw = [None] * 4
    qw[0] = load_quarter_d(0, q0_ab[0])

    def up_steps(q, b, at, aregs):
        (wa, war), (wb_, wbr), (wd, wdr), nf = qw[q]
        f0, f1 = QUARTERS[q]
        wav = wa[:, 0:8 * nf * 128].rearrange("p (k c) -> p k c", k=8)
        wbv = wb_[:, 0:8 * nf * 128].rearrange("p (k c) -> p k c", k=8)
        steps = []
        for fl in range(nf):
            def step(fl=fl):
                fc = f0 + fl
                ys = []
                for ab in range(2):
                    wv, wvr = (wav, war) if ab == 0 else (wbv, wbr)
                    ch = fc if ab == 0 else NFC + fc
                    pb, pr = pbank.next()
                    for k in range(8):
                        S.op("pe", lambda e, k=k: e.matmul(pb[:, :], lhsT=wv[:, k, fl * 128:(fl + 1) * 128],
                                                           rhs=H2T[:, k, b * 512:(b + 1) * 512], start=(k == 0), stop=(k == 7)),
                             reads=[wvr] + h2t_r[4 * b:4 * b + 4], writes=[pr], inc=(k == 7))
                    Ut, Ur = Ub.next()
                    S.op("act", lambda e: e.activation(out=Ut[:, 0:2], in_=halo[:, fl, ab, :], func=AF.Copy), reads=[R("halo")], writes=[Ur])
                    S.op("act", lambda e: e.activation(out=Ut[:, 2:514], in_=pb[:, :], func=AF.Copy), reads=[pr], writes=[Ur])
                    S.op("act", lambda e: e.activation(out=halo[:, fl, ab, :], in_=pb[:, 510:512], func=AF.Copy), reads=[pr], writes=[R("halo")])
                    Yt, Yr = (Ya.next() if ab == 0 else Yb.next())
                    S.op("act", lambda e: e.activation(out=Yt[:, :], in_=pb[:, :], func=AF.Identity, scale=cw[:, ch, 2:3], bias=cb[:, ch:ch + 1]),
                         reads=[pr, R("cw"), R("cb")], writes=[Yr])
                    S.op("dve", lambda e: e.scalar_tensor_tensor(out=Yt[:, :], in0=Ut[:, 1:513], scalar=cw[:, ch, 1:2], in1=Yt[:, :],
                                                                 op0=ALU.mult, op1=ALU.add),
                         reads=[Ur, Yr, R("cw")], writes=[Yr])
                    S.op("dve", lambda e: e.scalar_tensor_tensor(out=Yt[:, :], in0=Ut[:, 0:512], scalar=cw[:, ch, 0:1], in1=Yt[:, :],
                                                                 op0=ALU.mult, op1=ALU.add),
                         reads=[Ur, Yr, R("cw")], writes=[Yr])
                    ys.append((Yt, Yr))
                Sat, Sar = Sa.next()
                Y0, Y0r = ys[0]
                Y1, Y1r = ys[1]
                S.op("act", lambda e: e.activation(out=Sat[:, :], in_=Y0[:, :], func=AF.Silu), reads=[Y0r], writes=[Sar])
                S.op("pool", lambda e: e.tensor_tensor(out=at[:, fl, :], in0=Sat[:, :], in1=Y1[:, :], op=ALU.mult),
                     reads=[Sar, Y1r], writes=[aregs[fl]])
            steps.append(step)
        return steps

    def down_steps(q, b, at, aregs):
        (wa, war), (wb_, wbr), (wd, wdr), nf = qw[q]
        wdv = wd[:, 0:nf * 1024].rearrange("p (f c) -> p f c", f=nf)
        steps = []
        for j in range(4):
            for c in range(2):
                def step(j=j, c=c):
                    n = 4 * b + j
                    pb, pr = pbank.next()
                    for fl in range(nf):
                        S.op("pe", lambda e, fl=fl: e.matmul(pb[:, :], lhsT=at[:, fl, j * 128:(j + 1) * 128],
                                                             rhs=wdv[:, fl, c * 512:(c + 1) * 512], start=(fl == 0), stop=(fl == nf - 1)),
                             reads=[aregs[fl], wdr], writes=[pr], inc=(fl == nf - 1))
                    S.op("dve", lambda e: e.tensor_tensor(out=X1[:, n, c * 512:(c + 1) * 512], in0=X1[:, n, c * 512:(c + 1) * 512],
                                                          in1=pb[:, :], op=ALU.add),
                         reads=[x1_r[n], pr], writes=[x1_r[n]])
                steps.append(step)
        if q == 3:
            def fin():
                nl = [4 * b + j for j in range(4)]
                stats(nl, ssf, rsf, R("stf"), ostg.items[0][0], ostg.items[0][1], x1_r)
                for n in nl:
                    ot, orl = ostg2.next()
                    orr = orl[0]
                    guard = []
                    if orl[1] is not None:
                        guard = [orl[1]]
                        orl[1] = None
                    S.op("dve", lambda e, n=n: e.scalar_tensor_tensor(out=ot, in0=X1[:, n, :], scalar=rsf[:, n:n + 1], in1=gfb[:, :],
                                                                      op0=ALU.mult, op1=ALU.mult),
                         reads=[x1_r[n], R("stf"), R("gfb")], writes=[orr] + guard)
                    S.dma("sp", lambda e, n=n: e.dma_start(out=out_d[n * 128:(n + 1) * 128, :], in_=ot), orl[2], reads=[orr])
            steps.append(fin)
        return steps

    _stg = []
    for k in range(3):
        for si in range(2):
            rt, rr = ring2[si]
            _stg.append((rt[:, k * 2048:(k + 1) * 2048].bitcast(F32), [Reg("ostg2_%d_%d" % (si, k)), rr, S.new_dma_sem()]))
    ostg2 = Rot(_stg)
    pending_down = []
    for q in range(4):
        S.op("dve", lambda e: e.memset(halo[:, :, :, :], 0.0), writes=[R("halo")])
        for b in range(4):
            at, aregs = actb.next()
            ups = up_steps(q, b, at, aregs)
            downs = pending_down
            nd = len(downs)
            per = -(-nd // max(1, len(ups) - 1)) if nd else 0
            di = 0
            for ui, u in enumerate(ups):
                u()
                if ui >= 1:
                    for _ in range(per):
                        if di < nd:
                            downs[di]()
                            di += 1
            while di < nd:
                downs[di]()
                di += 1
            if b == 0 and q + 1 < 4:
                qw[q + 1] = load_quarter_d(q + 1, load_quarter_ab(q + 1))
            pending_down = down_steps(q, b, at, aregs)
    for d_ in pending_down:
        d_()

    if debug:
        pass
    for _t, _l in ostg2.items:
        S.wait_all_dma("sp", _l[2])
    S.build()
    return nc


_CACHE = {}


def _consts():
    f32 = np.float32
    half = 64
    inv_freq = np.power(f32(10000.0), -np.arange(half, dtype=f32) / f32(half)).astype(f32)
    ang = (np.arange(S_LEN, dtype=f32)[:, None] * inv_freq[None, :]).astype(f32)
    cos = np.cos(ang).astype(f32)
    sin = np.sin(ang).astype(f32)
    C = np.concatenate([cos, cos], -1)
    Sg = np.concatenate([-sin, sin], -1)
    rotc = np.ascontiguousarray(C.reshape(NCH, 128, 128).transpose(1, 0, 2))
    rots = np.ascontiguousarray(Sg.reshape(NCH, 128, 128).transpose(1, 0, 2))
    gam = (1.0 - np.power(2.0, -5.0 - np.arange(4))).astype(np.float64)
    pos = np.arange(128, dtype=np.float64)
    cd = gam ** 128
    kd = (gam[None, :] ** (127.0 - pos)[:, None]) * cd[None, :] * (128.0 ** -0.5)
    kd = np.broadcast_to(kd[:, :, None], (128, 4, 128)).astype(f32).copy()
    tri = (pos[None, :] >= pos[:, None])
    mask = np.where(tri[:, None, :], (1.0 / cd)[None, :, None], 0.0).astype(f32)
    epsr = 128.0 * EPS * (gam[:, None] ** (2.0 * (127.0 - pos))[None, :])
    epsr = np.broadcast_to(epsr[None], (128, 4, 128)).astype(f32).copy()
    tril = np.broadcast_to(tri[:, None, :], (128, 4, 128)).astype(f32).copy()
    ident = np.eye(128, dtype=f32)
    kdh = kd[:, :, 0]
    ckq = np.ascontiguousarray((rotc[:, :, None, :] * kdh[:, None, :, None]).astype(f32))
    skq = np.ascontiguousarray((rots[:, :, None, :] * kdh[:, None, :, None]).astype(f32))
    return dict(rotc=rotc, rots=rots, ckq=ckq, skq=skq, mask=mask, epsr=epsr, tril=tril, ident=ident)


def _col(v, k):
    return np.ascontiguousarray(np.asarray(v, np.float32).reshape(k, 128).T)


def kernel(x, mix_norm_g, w_in, ret_norm_g, sgu_ln_g, sgu_ln_b, sgu_w_s, sgu_b_s,
           w_out, ffn_norm_g, w_up, conv_w, conv_b, w_down, final_norm_g):
    debug = bool(os.environ.get("MK_DEBUG"))
    key = ("nc", debug)
    if key not in _CACHE:
        _CACHE[key] = build_program(debug)
    nc = _CACHE[key]
    f32 = np.float32
    x = np.asarray(x, f32)
    shared = dict(_consts())
    shared.update(
        w_in=np.ascontiguousarray(np.asarray(w_in, f32)[0]),
        w_out=np.ascontiguousarray(np.asarray(w_out, f32)[0]),
        w_up=np.ascontiguousarray(np.asarray(w_up, f32)[0]),
        w_down=np.ascontiguousarray(np.asarray(w_down, f32)[0]),
        g1c=_col(np.asarray(mix_norm_g)[0], 8),
        g2c=_col(np.asarray(ffn_norm_g)[0], 8),
        gf=np.ascontiguousarray(np.asarray(final_norm_g, f32)),
        gret=_col(np.asarray(ret_norm_g)[0], 4),
        lng=np.ascontiguousarray(np.asarray(sgu_ln_g, f32)[0].T),
        lnb=np.ascontiguousarray(np.asarray(sgu_ln_b, f32)[0].T),
        wst=np.ascontiguousarray(np.asarray(sgu_w_s, f32)[0].transpose(2, 0, 1)),
        bs=np.ascontiguousarray(np.asarray(sgu_b_s, f32)[0].reshape(512)),
        cw=np.ascontiguousarray(np.asarray(conv_w, f32)[0].reshape(3, 44, 128).transpose(2, 1, 0)),
        cb=_col(np.asarray(conv_b)[0], 44),
    )
    in_maps = []
    for c in range(8):
        m = dict(shared)
        m["x"] = np.ascontiguousarray(x[c])
        in_maps.append(m)
    res = run_bass_kernel_spmd(nc, in_maps, core_ids=list(range(8)))
    out = np.stack([np.asarray(r["out"], f32) for r in res.results], axis=0)
    if debug:
        kernel.last_dbg = [r.get("dbg") for r in res.results]
    return out
```

```python
import os
import contextlib
import types
import numpy as np
import concourse.bass as bass
import concourse.mybir as mybir
from concourse.bass_utils import run_bass_kernel_spmd

F32 = mybir.dt.float32
BF16 = mybir.dt.bfloat16
AF = mybir.ActivationFunctionType
ALU = mybir.AluOpType

D = 1024
S_LEN = 2048
NCH = 16
DFF = 2816
NFC = 22
EPS = 1e-6
QUARTERS = [(0, 5), (5, 10), (10, 16), (16, 22)]
SB_BASE = 16512
SB_END = 229376


class Reg:
    __slots__ = ("name", "w", "r")

    def __init__(self, name):
        self.name = name
        self.w = None
        self.r = {}


def _snap(fn):
    if fn.__closure__ is None:
        return fn
    cells = []
    for c in fn.__closure__:
        try:
            cells.append(types.CellType(c.cell_contents))
        except ValueError:
            cells.append(c)
    g = types.FunctionType(fn.__code__, fn.__globals__, fn.__name__, fn.__defaults__, tuple(cells))
    g.__kwdefaults__ = fn.__kwdefaults__
    return g


class Sched:
    ENGS = ("pe", "act", "dve", "pool", "sp")

    def __init__(self, nc):
        self.nc = nc
        self.prog = {e: [] for e in self.ENGS}
        self.cnt = {e: 0 for e in self.ENGS}
        self.known = {e: {} for e in self.ENGS}
        self.dma_sems = []

    def new_dma_sem(self):
        self.dma_sems.append(0)
        return len(self.dma_sems) - 1

    def _need(self, eng, deps, key, val):
        if self.known[eng].get(key, 0) >= val:
            return
        if deps.get(key, 0) < val:
            deps[key] = val

    def _add_dep(self, eng, deps, tok):
        if tok[0] == "dma":
            self._need(eng, deps, ("dma", tok[1]), tok[2])
        else:
            e, s = tok
            if e == eng and e == "pe":
                return
            self._need(eng, deps, e, s)

    def _collect(self, eng, reads, writes, disjoint=()):
        deps = {}
        for r in reads:
            if r.w is not None:
                self._add_dep(eng, deps, r.w)
        for w in writes:
            if w.w is not None and not (w in disjoint and w.w[0] == eng):
                self._add_dep(eng, deps, w.w)
            for k, v in w.r.items():
                if isinstance(k, tuple):
                    self._add_dep(eng, deps, ("dma", k[1], v))
                else:
                    self._add_dep(eng, deps, (k, v))
        return deps

    def _emit_waits(self, eng, deps):
        for key, val in deps.items():
            self.known[eng][key] = val
            self.prog[eng].append(("wait", key, val))

    def op(self, eng, fn, reads=(), writes=(), inc=True, disjoint=()):
        deps = self._collect(eng, reads, writes, disjoint)
        self._emit_waits(eng, deps)
        if inc:
            self.cnt[eng] += 1
            seq = self.cnt[eng]
        else:
            seq = self.cnt[eng] + 1
        self.prog[eng].append(("op", _snap(fn), inc))
        for r in reads:
            if r.r.get(eng, 0) < seq:
                r.r[eng] = seq
        for w in writes:
            w.w = (eng, seq)
            w.r = {}
        return seq

    def dma(self, eng, fn, sem, reads=(), writes=()):
        deps = self._collect(eng, reads, writes)
        self._emit_waits(eng, deps)
        self.dma_sems[sem] += 16
        cnt = self.dma_sems[sem]
        self.prog[eng].append(("dma", _snap(fn), sem))
        for r in reads:
            r.r[("dma", sem)] = cnt
        for w in writes:
            w.w = ("dma", sem, cnt)
            w.r = {}
        return cnt

    def wait_all_dma(self, eng, sem):
        self.prog[eng].append(("wait", ("dma", sem), self.dma_sems[sem]))

    def build(self):
        nc = self.nc
        with contextlib.ExitStack() as st:
            esem = {e: st.enter_context(nc.semaphore("s_" + e)) for e in self.ENGS}
            dsem = [st.enter_context(nc.semaphore("d_%d" % i)) for i in range(len(self.dma_sems))]
            block = st.enter_context(nc.Block())

            def replay(e, engine):
                for item in self.prog[e]:
                    if item[0] == "wait":
                        key, val = item[1], item[2]
                        if isinstance(key, tuple):
                            engine.wait_ge(dsem[key[1]], val)
                        else:
                            engine.wait_ge(esem[key], val)
                    elif item[0] == "op":
                        ins = item[1](engine)
                        if item[2]:
                            ins.then_inc(esem[e], 1)
                    else:
                        ins = item[1](engine)
                        ins.then_inc(dsem[item[2]], 16)

            @block.tensor
            def _(eng):
                replay("pe", eng)

            @block.scalar
            def _(eng):
                replay("act", eng)

            @block.vector
            def _(eng):
                replay("dve", eng)

            @block.gpsimd
            def _(eng):
                replay("pool", eng)

            @block.sync
            def _(eng):
                replay("sp", eng)


class Arena:
    def __init__(self, nc, start, end, tag):
        self.nc, self.off, self.end, self.tag = nc, start, end, tag
        self.n = 0

    def alloc(self, name, shape, dt):
        nbytes = int(np.prod(shape[1:])) * (4 if dt == F32 else 2)
        nbytes = (nbytes + 31) // 32 * 32
        assert self.off + nbytes <= self.end, (self.tag, name, self.off, nbytes, self.end)
        t = self.nc.alloc_sbuf_tensor_at("%s_%s" % (self.tag, name), list(shape), dt, offset=self.off)
        self.off += nbytes
        return t


class SafeRot:
    def __init__(self, items):
        self.items, self.i = items, 0
        self.open = [False] * len(items)

    def next(self):
        for _ in range(len(self.items)):
            k = self.i % len(self.items)
            self.i += 1
            if not self.open[k]:
                self.open[k] = True
                return self.items[k]
        raise AssertionError("SafeRot exhausted")

    def release(self, item):
        for k, it in enumerate(self.items):
            if it is item or it[1] is item[1]:
                self.open[k] = False
                return
        raise AssertionError("release of unknown item")


class Rot:
    def __init__(self, items):
        self.items, self.i = items, 0

    def next(self):
        it = self.items[self.i % len(self.items)]
        self.i += 1
        return it


def build_program(debug=False):
    nc = bass.Bass("TRN2", target_bir_lowering=False)
    S = Sched(nc)

    def din(name, shape):
        return nc.dram_tensor(name, list(shape), F32, kind="ExternalInput").ap()

    x_d = din("x", [S_LEN, D])
    w_in_d = din("w_in", [D, 3072])
    w_out_d = din("w_out", [D, D])
    w_up_d = din("w_up", [D, 2 * DFF])
    w_down_d = din("w_down", [DFF, D])
    g1c_d = din("g1c", [128, 8])
    g2c_d = din("g2c", [128, 8])
    gf_d = din("gf", [D])
    gret_d = din("gret", [128, 4])
    lng_d = din("lng", [128, 4])
    lnb_d = din("lnb", [128, 4])
    wst_d = din("wst", [128, 4, 128])
    bs_d = din("bs", [512])
    cw_d = din("cw", [128, 44, 3])
    cb_d = din("cb", [128, 44])
    rotc_d = din("rotc", [128, NCH, 128])
    rots_d = din("rots", [128, NCH, 128])
    ckq_d = din("ckq", [128, NCH, 4, 128])
    skq_d = din("skq", [128, NCH, 4, 128])
    mask_d = din("mask", [128, 4, 128])
    epsr_d = din("epsr", [128, 4, 128])
    tril_d = din("tril", [128, 4, 128])
    ident_d = din("ident", [128, 128])
    out_d = nc.dram_tensor("out", [S_LEN, D], F32, kind="ExternalOutput").ap()
    dbg_d = None
    if debug:
        dbg_d = nc.dram_tensor("dbg", [S_LEN, D], F32, kind="ExternalOutput").ap()

    top = Arena(nc, SB_BASE, SB_END, "t")
    X1 = top.alloc("X1", [128, NCH, D], F32)
    H2T = top.alloc("H2T", [128, 8, S_LEN], BF16)
    identb = top.alloc("identb", [128, 128], BF16)
    onesb = top.alloc("onesb", [128, 128], BF16)
    g1c = top.alloc("g1c", [128, 8], F32)
    g2c = top.alloc("g2c", [128, 8], F32)
    cw = top.alloc("cw", [128, 44, 3], F32)
    cb = top.alloc("cb", [128, 44], F32)
    ss1 = top.alloc("ss1", [128, NCH], F32)
    rs1 = top.alloc("rs1", [128, NCH], F32)
    ss2 = top.alloc("ss2", [128, NCH], F32)
    rs2 = top.alloc("rs2", [128, NCH], F32)
    ssf = top.alloc("ssf", [128, NCH], F32)
    rsf = top.alloc("rsf", [128, NCH], F32)
    epsb = top.alloc("epsb", [128, 2], F32)
    bar_s = top.alloc("bar_s", [128, 8], F32)
    R_START = top.off
    a1 = Arena(nc, R_START, SB_END, "m1")
    a2 = Arena(nc, R_START, SB_END, "m2")

    hT = a1.alloc("hT", [128, 8, 512], BF16)
    rA = Rot([(a1.alloc("rA%d" % i, [128, 4, 128], F32), Reg("rA%d" % i)) for i in range(1)])
    rB = Rot([(a1.alloc("rB%d" % i, [128, 4, 128], F32), Reg("rB%d" % i)) for i in range(1)])
    ktab = Rot([((a1.alloc("ckt%d" % i, [128, 4, 128], F32), a1.alloc("skt%d" % i, [128, 4, 128], F32)), (Reg("ckt%d" % i), Reg("skt%d" % i))) for i in range(2)])
    rotc = a1.alloc("rotc", [128, 4, 128], F32)
    rots = a1.alloc("rots", [128, 4, 128], F32)
    ring1 = [(a1.alloc("ring%d" % i, [128, 8, 512], BF16), Reg("ring1_%d" % i)) for i in range(2)]
    identf = a1.alloc("identf", [128, 128], F32)
    maskt = a1.alloc("mask", [128, 4, 128], F32)
    epsbf = a1.alloc("epsbf", [128, 512], BF16)
    wtm = a1.alloc("wtm", [128, 4, 128], BF16)
    bias2 = a1.alloc("bias2", [128, 4, 128], F32)
    gret = a1.alloc("gret", [128, 4], F32)
    lng = a1.alloc("lng", [128, 4], F32)
    lnb = a1.alloc("lnb", [128, 4], F32)
    hn = Rot([(a1.alloc("hn%d" % i, [128, D], BF16), Reg("hn%d" % i)) for i in range(2)])
    hT_r = [Reg("hT_%d" % j) for j in range(4)]
    qtm = Rot([(a1.alloc("qtm%d" % i, [128, 4, 128], BF16), Reg("qtm%d" % i)) for i in range(2)])
    ktm = Rot([(a1.alloc("ktm%d" % i, [128, 4, 128], BF16), Reg("ktm%d" % i)) for i in range(4)])
    vtm = Rot([(a1.alloc("vtm%d" % i, [128, 4, 128], BF16), Reg("vtm%d" % i)) for i in range(4)])
    ztm = Rot([(a1.alloc("ztm%d" % i, [128, 4, 128], BF16), Reg("ztm%d" % i)) for i in range(4)])
    qT = Rot([(a1.alloc("qT%d" % i, [128, 4, 128], BF16), Reg("qT%d" % i)) for i in range(4)])
    kT = Rot([(a1.alloc("kT%d" % i, [128, 4, 128], BF16), Reg("kT%d" % i)) for i in range(4)])
    sg = a1.alloc("sg", [128, 4, 512], BF16)
    gu = a1.alloc("gu", [128, 4, 512], BF16)
    sg_r = [Reg("sg%d" % h) for h in range(4)]
    gu_r = [Reg("gu%d" % h) for h in range(4)]
    st_sgu = a1.alloc("st_sgu", [128, 7, 16], F32)
    smb = Rot([(a1.alloc("sm%d" % i, [128, 4, 128], BF16), Reg("sm%d" % i)) for i in range(2)])
    Sst = a1.alloc("Sst", [128, 4, 128], F32)
    Sbf = Rot([(a1.alloc("Sbf%d" % i, [128, 4, 128], BF16), Reg("Sbf%d" % i)) for i in range(2)])
    sqb2 = Rot([(a1.alloc("sqc%d" % i, [128, 512], BF16), Reg("sqc%d" % i)) for i in range(2)])
    rsb = a1.alloc("rsb", [128, 512], F32)
    o1 = a1.alloc("o1", [128, 4, 128], F32)
    mx = a1.alloc("mx", [128, 4, 128], F32)
    rsbB = Rot([(rsb, Reg("rsb")), (a1.alloc("rsb2", [128, 512], F32), Reg("rsb2"))])
    mxB = Rot([(mx, None), (a1.alloc("mxb", [128, 4, 128], F32), Reg("mxb"))])
    th = o1[:, :, :].rearrange("p h q -> p (h q)")
    gvb = Rot([(a1.alloc("gv%d" % i, [128, 4, 128], BF16), Reg("gv%d" % i)) for i in range(4)])
    catb = Rot([(a1.alloc("cat%d" % i, [128, 8, 128], BF16), Reg("cat%d" % i)) for i in range(2)])

    ring2 = [(a2.alloc("ring%d" % i, [128, 6144], BF16), Reg("ring2_%d" % i)) for i in range(6)]
    gfb = a2.alloc("gfb", [128, D], F32)
    halo = a2.alloc("halo", [128, 12, 2, 2], F32)
    Ub = Rot([(a2.alloc("U%d" % i, [128, 514], F32), Reg("U%d" % i)) for i in range(3)])
    Ya = Rot([(a2.alloc("Ya%d" % i, [128, 512], F32), Reg("Ya%d" % i)) for i in range(2)])
    Yb = Rot([(a2.alloc("Yb%d" % i, [128, 512], F32), Reg("Yb%d" % i)) for i in range(2)])
    Sa = Rot([(a2.alloc("Sa%d" % i, [128, 512], F32), Reg("Sa%d" % i)) for i in range(1)])
    actb = Rot([(a2.alloc("act%d" % i, [128, 6, 512], BF16), [Reg("act%d_%d" % (i, f)) for f in range(6)]) for i in range(2)])
    ostg = Rot([(a2.alloc("ostg%d" % i, [128, D], F32), Reg("ostg%d" % i)) for i in range(1)])

    pbank = Rot([(nc.alloc_psum_tensor("pb%d" % i, [128, 512], F32), Reg("pb%d" % i)) for i in range(6)])
    ptb = Rot([(nc.alloc_psum_tensor("pt%d" % i, [128, 1024], BF16), Reg("pt%d" % i)) for i in range(2)])
    pbank_all = pbank
    pbank4 = SafeRot(pbank.items[0:4])
    pbo2 = Rot(pbank.items[4:6])

    RG = {}

    def R(name):
        if name not in RG:
            RG[name] = Reg(name)
        return RG[name]

    x1_r = [Reg("x1_%d" % n) for n in range(NCH)]
    h2t_r = [Reg("h2t_%d" % n) for n in range(NCH)]
    mp1_all = Reg("mp1_all")

    sem_c = S.new_dma_sem()
    sem_x = [S.new_dma_sem() for _ in range(NCH)]
    sem_rot = S.new_dma_sem()
    sem_rot2 = S.new_dma_sem()
    sem_ring1 = [S.new_dma_sem() for _ in range(2)]
    sem_ring2 = [S.new_dma_sem() for _ in range(6)]
    sem_out = S.new_dma_sem()
    sem_c2 = S.new_dma_sem()
    ktab_sems = [(S.new_dma_sem(), S.new_dma_sem()) for _ in range(2)]

    def ld(eng, dst, src, reg, sem=sem_c, extra_w=()):
        S.dma(eng, lambda e: e.dma_start(out=dst, in_=src), sem, writes=[reg] + list(extra_w))

    sem_c0 = S.new_dma_sem()
    ld("act", identf[:, :], ident_d[:, :], R("identf"), sem=sem_c0)
    ld("act", g1c[:, :], g1c_d[:, :], R("g1c"), sem=sem_c0)
    for _nm in ("identf", "g1c"):
        RG[_nm].w = ("dma", sem_c0, S.dma_sems[sem_c0])
    def load_x_block(b, gate=()):
        for j in range(4):
            n = 4 * b + j
            S.dma("sp", lambda e: e.dma_start(out=X1[:, n, :], in_=x_d[n * 128:(n + 1) * 128, :]), sem_x[n],
                  reads=list(gate), writes=[x1_r[n]])

    for b in range(1):
        load_x_block(0)
        if b == 0:
            ld("act", g2c[:, :], g2c_d[:, :], R("g2c"))
            ld("act", maskt[:, :, :], mask_d[:, :, :], R("mask"))
            ld("act", gret[:, :], gret_d[:, :], R("gret"))
            ld("act", lng[:, :], lng_d[:, :], R("lng"))
            ld("act", lnb[:, :], lnb_d[:, :], R("lnb"))
            ld("act", cw[:, :, :], cw_d[:, :, :], R("cw"))
            ld("act", cb[:, :], cb_d[:, :], R("cb"))
            ld("act", o1[:, :, :], wst_d[:, :, :], R("o1"))
            ld("act", mx[:, :, :], tril_d[:, :, :], R("mx"))
            ld("act", rsb[:, :], bs_d.partition_broadcast(128), R("rsb"))

    for _nm in ("g2c", "mask", "gret", "lng", "lnb", "cw", "cb", "o1", "mx", "rsb"):
        RG[_nm].w = ("dma", sem_c, S.dma_sems[sem_c])
    S.op("dve", lambda e: e.tensor_copy(out=identb[:, :], in_=identf[:, :]), reads=[R("identf")], writes=[R("identb")])
    S.op("dve", lambda e: e.memset(onesb[:, :], 1.0), writes=[R("onesb")])
    S.op("dve", lambda e: e.memset(Sst[:, :, :], 0.0), writes=[R("Sst%d" % h) for h in range(4)])
    sbf_t, sbf_r = Sbf.next()
    S.op("dve", lambda e, t=sbf_t: e.memset(t[:, :, :], 0.0), writes=[sbf_r])
    cur_sbf = (sbf_t, sbf_r)

    slab_issued = [0]

    def slab_src(i):
        c = i % 8
        if c < 6:
            return w_in_slab(SLAB_ORDER[c])
        return w_out_slab(c - 6)

    def slab_issue_upto(i):
        while slab_issued[0] <= min(i, 31):
            k = slab_issued[0]
            t, r = ring1[k % 2]
            src = slab_src(k)
            S.dma("pool", lambda e: e.dma_start(out=t[:, :, :], in_=src), sem_ring1[k % 2], writes=[r])
            slab_issued[0] += 1

    def slab_get(i):
        slab_issue_upto(i)
        return ring1[i % 2]

    def slab_consumed(i):
        slab_issue_upto(i + 2)


    def w_in_slab(c):
        return w_in_d[:, c * 512:(c + 1) * 512].rearrange("(k p) c -> p k c", p=128)

    def w_out_slab(c):
        return w_out_d[:, c * 512:(c + 1) * 512].rearrange("(k p) c -> p k c", p=128)

    SLAB_ORDER = [0, 5, 1, 2, 3, 4]

    def norm_to_T(n, j, src_stat, rstat, gcol, greg, dstT, dst_col0, dst_reg, stat_reg):
        ht, hr = hn.next()
        S.op("act", lambda e: e.activation(out=ht[:, :], in_=X1[:, n, :], func=AF.Identity, scale=rstat[:, n:n + 1]),
             reads=[x1_r[n], stat_reg], writes=[hr])
        pt, ptr = ptb.next()
        for k in range(8):
            S.op("pe", lambda e, k=k: e.transpose(out=pt[:, k * 128:(k + 1) * 128], in_=ht[:, k * 128:(k + 1) * 128], identity=identb[:, :]),
                 reads=[hr, R("identb")], writes=[ptr], inc=(k == 7))
        S.op("dve", lambda e: e.tensor_tensor(out=dstT[:, :, dst_col0:dst_col0 + 128], in0=pt[:, :].rearrange("p (k t) -> p k t", k=8),
                                              in1=gcol[:, :].rearrange("p (k o) -> p k o", o=1).to_broadcast([128, 8, 128]), op=ALU.mult),
             reads=[ptr, greg], writes=[dst_reg])

    def stats(nlist, sstat, rstat, stat_reg, junk, junk_reg, xregs):
        for n in nlist:
            if junk is None:
                jt, jr = hn.next()
            else:
                jt, jr = junk, junk_reg
            S.op("act", lambda e, n=n, jt=jt: e.activation(out=jt[:, :], in_=X1[:, n, :], func=AF.Square, accum_out=sstat[:, n:n + 1]),
                 reads=[xregs[n]], writes=[jr, stat_reg])
        n0, n1 = nlist[0], nlist[-1] + 1
        S.op("act", lambda e: e.activation(out=rstat[:, n0:n1], in_=sstat[:, n0:n1], func=AF.Ln, scale=1.0 / D, bias=epsb[:, 0:1]),
             reads=[stat_reg, R("epsb")], writes=[stat_reg])
        S.op("act", lambda e: e.activation(out=rstat[:, n0:n1], in_=rstat[:, n0:n1], func=AF.Exp, scale=-0.5),
             reads=[stat_reg], writes=[stat_reg])

    S.op("dve", lambda e: e.memset(epsb[:, 0:1], EPS), writes=[R("epsb")])
    S.op("dve", lambda e: e.memset(epsb[:, 1:2], 0.0), writes=[R("epsb")])

    dbg_taps = []

    sem_eps = S.new_dma_sem()
    S.dma("pool", lambda e: e.dma_start(out=epsbf[:, :], in_=epsr_d.rearrange("p h q -> p (h q)")), sem_eps, writes=[R("epsbf")])

    def setup_late():
        S.op("dve", lambda e: e.tensor_scalar(out=gret[:, :], in0=gret[:, :], scalar1=0.5, scalar2=None, op0=ALU.mult),
             reads=[R("gret")], writes=[R("gret")])
        S.op("dve", lambda e: e.tensor_tensor(out=wtm[:, :, :], in0=o1[:, :, :], in1=mx[:, :, :], op=ALU.mult),
             reads=[R("o1"), R("mx")], writes=[R("wtm")])
        pb, pr = pbank.next()
        S.op("pe", lambda e: e.matmul(pb[:, :], lhsT=onesb[:, :], rhs=wtm[:, :, :].rearrange("p g t -> p (g t)"), start=True, stop=True),
             reads=[R("onesb"), R("wtm")], writes=[pr])
        for g in range(4):
            S.op("dve", lambda e, g=g, pb=pb: e.scalar_tensor_tensor(out=bias2[:, g, :], in0=pb[:, g * 128:(g + 1) * 128], scalar=lnb[:, g:g + 1],
                                                                     in1=rsb[:, g * 128:(g + 1) * 128], op0=ALU.mult, op1=ALU.add),
                 reads=[pr, R("lnb"), R("rsb")], writes=[R("bias2")])

    ring2_i = [0]

    def load_quarter_ab(q, guardA=(), guardB=()):
        f0, f1 = QUARTERS[q]
        nf = f1 - f0
        wa, war = ring2[ring2_i[0] % 6]
        i = ring2_i[0] % 6
        ring2_i[0] += 1
        S.dma("pool", lambda e: e.dma_start(out=wa[:, 0:8 * nf * 128].rearrange("p (k c) -> p k c", k=8),
                                            in_=w_up_d[:, f0 * 128:f1 * 128].rearrange("(k p) c -> p k c", p=128)),
              sem_ring2[i], writes=[war] + list(guardA))
        wb_, wbr = ring2[ring2_i[0] % 6]
        i2 = ring2_i[0] % 6
        ring2_i[0] += 1
        S.dma("pool", lambda e: e.dma_start(out=wb_[:, 0:8 * nf * 128].rearrange("p (k c) -> p k c", k=8),
                                            in_=w_up_d[:, DFF + f0 * 128:DFF + f1 * 128].rearrange("(k p) c -> p k c", p=128)),
              sem_ring2[i2], writes=[wbr] + list(guardB))
        return (wa, war), (wb_, wbr), nf

    def load_quarter_d(q, ab):
        (wa, war), (wb_, wbr), nf = ab
        f0, f1 = QUARTERS[q]
        wd, wdr = ring2[ring2_i[0] % 6]
        i3 = ring2_i[0] % 6
        ring2_i[0] += 1
        S.dma("pool", lambda e: e.dma_start(out=wd[:, 0:nf * 1024].rearrange("p (f c) -> p f c", f=nf),
                                            in_=w_down_d[f0 * 128:f1 * 128, :].rearrange("(f p) c -> p f c", p=128)),
              sem_ring2[i3], writes=[wdr])
        return (wa, war), (wb_, wbr), (wd, wdr), nf

    q0_ab = [None]

    deferred = []

    def defer(n, thunk):
        deferred.append([n, thunk])

    def pe_tick():
        for it in deferred:
            it[0] -= 1
        for it in list(deferred):
            if it[0] <= 0:
                deferred.remove(it)
                it[1]()

    def flush_deferred():
        while deferred:
            deferred.pop(0)[1]()

    def phase_A(b):
        nl = [4 * b + j for j in range(4)]
        for j in range(4):
            norm_to_T(nl[j], j, ss1, rs1, g1c, R("g1c"), hT, j * 128, hT_r[j], R("st1"))

    slab_issue_upto(1)
    stats([0, 1, 2, 3], ss1, rs1, R("st1"), None, None, x1_r)
    phase_A(0)
    setup_late()
    for b in range(4):
        nl = [4 * b + j for j in range(4)]
        S.dma("sp", lambda e: e.dma_start(out=rotc[:, :, :], in_=rotc_d[:, 4 * b:4 * b + 4, :]), sem_rot, writes=[R("rotc")])
        S.dma("sp", lambda e: e.dma_start(out=rots[:, :, :], in_=rots_d[:, 4 * b:4 * b + 4, :]), sem_rot2, writes=[R("rots")])

        if b + 1 < 4:
            load_x_block(b + 1, gate=[hT_r[3]])
            stats([4 * (b + 1) + j for j in range(4)], ss1, rs1, R("st1"), None, None, x1_r)
        chunk = [dict() for _ in range(4)]
        for ci, c in enumerate(SLAB_ORDER[:4]):
            wt, wr = slab_get(8 * b + ci)
            for j in range(4):
                pb, pr = pbank.next()
                for k in range(8):
                    S.op("pe", lambda e, k=k: e.matmul(pb[:, :], lhsT=hT[:, k, j * 128:(j + 1) * 128], rhs=wt[:, k, :],
                                                       start=(k == 0), stop=(k == 7)),
                         reads=[hT_r[j], wr], writes=[pr], inc=(k == 7))
                pv = pb[:, :].rearrange("p (h d) -> p h d", h=4)
                n = nl[j]
                if c in (0, 1):
                    if c == 0:
                        ctab = rotc[:, j:j + 1, :].to_broadcast([128, 4, 128])
                        stab = rots[:, j:j + 1, :].rearrange("p o (two d) -> p o two d", two=2).to_broadcast([128, 4, 2, 64])
                        tabs = [R("rotc"), R("rots")]
                    else:
                        (ckt, skt), (ckr, skr) = ktab.next()
                        ksem_c, ksem_s = ktab_sems[(ktab.i - 1) % 2]
                        S.dma("sp", lambda e: e.dma_start(out=ckt[:, :, :], in_=ckq_d[:, n, :, :]), ksem_c, writes=[ckr])
                        S.dma("sp", lambda e: e.dma_start(out=skt[:, :, :], in_=skq_d[:, n, :, :]), ksem_s, writes=[skr])
                        ctab = ckt[:, :, :]
                        stab = skt[:, :, :].rearrange("p h (two d) -> p h two d", two=2)
                        tabs = [ckr, skr]
                    At, Ar = rA.next()
                    Bt, Br = rB.next()
                    S.op("dve", lambda e: e.tensor_tensor(out=At[:, :, :], in0=pv, in1=ctab, op=ALU.mult),
                         reads=[pr] + tabs, writes=[Ar])
                    pvsw = pb[:, :].rearrange("p (h two d) -> p h two d", h=4, two=2)[:, :, ::-1, :]
                    S.op("dve", lambda e: e.tensor_tensor(out=Bt[:, :, :].rearrange("p h (two d) -> p h two d", two=2), in0=pvsw, in1=stab, op=ALU.mult),
                         reads=[pr] + tabs, writes=[Br])
                    dt_, dr_ = (qtm.next() if c == 0 else ktm.next())
                    S.op("dve", lambda e: e.tensor_tensor(out=dt_[:, :, :], in0=At[:, :, :], in1=Bt[:, :, :], op=ALU.add),
                         reads=[Ar, Br], writes=[dr_])
                    Tt, Tr = (qT.next() if c == 0 else kT.next())

                    def do_T(dt_=dt_, dr_=dr_, Tt=Tt, Tr=Tr):
                        pt, ptr = ptb.next()
                        for h in range(4):
                            S.op("pe", lambda e, h=h: e.transpose(out=pt[:, h * 128:(h + 1) * 128], in_=dt_[:, h, :], identity=identb[:, :]),
                                 reads=[dr_, R("identb")], writes=[ptr], inc=(h == 3))
                        S.op("act", lambda e: e.activation(out=Tt[:, :, :].rearrange("p h t -> p (h t)"), in_=pt[:, 0:512], func=AF.Copy),
                             reads=[ptr], writes=[Tr])
                    defer(2, do_T)
                    chunk[j]["q" if c == 0 else "k"] = (dt_, dr_)
                    chunk[j]["qT" if c == 0 else "kT"] = (Tt, Tr)
                elif c == 2:
                    vt, vr = vtm.next()
                    S.op("act", lambda e: e.activation(out=vt[:, :, :].rearrange("p h d -> p (h d)"), in_=pb[:, :], func=AF.Copy),
                         reads=[pr], writes=[vr])
                    chunk[j]["v"] = (vt, vr)
                else:
                    gvt, gvr = gvb.next()
                    for g in range(4):
                        S.op("act", lambda e, g=g: e.activation(out=gvt[:, g, :], in_=pv[:, g, :], func=AF.Gelu,
                                                                accum_out=st_sgu[:, 0, j * 4 + g:j * 4 + g + 1]),
                             reads=[pr], writes=[gvr, R("st_s1")], disjoint=[gvr, R("st_s1")])
                    for g in range(4):
                        sqj, sqjr = sqb2.items[0]
                        S.op("dve", lambda e, g=g: e.scalar_tensor_tensor(out=sqj[:, g * 128:(g + 1) * 128], in0=gvt[:, g, :], scalar=1.0, in1=gvt[:, g, :],
                                                                          op0=ALU.mult, op1=ALU.mult,
                                                                          accum_out=st_sgu[:, 1, j * 4 + g:j * 4 + g + 1]),
                             reads=[gvr], writes=[sqjr, R("st_s2")], disjoint=[sqjr, R("st_s2")])
                    chunk[j]["gv"] = (gvt, gvr)
                pe_tick()
            slab_consumed(8 * b + ci)
        S.op("dve", lambda e: e.tensor_scalar(out=st_sgu[:, 2, :], in0=st_sgu[:, 0, :], scalar1=1.0 / 128, scalar2=None, op0=ALU.mult),
             reads=[R("st_sgu"), R("st_s1"), R("st_s2")], writes=[R("st_sgu")])
        S.op("dve", lambda e: e.tensor_tensor(out=st_sgu[:, 3, :], in0=st_sgu[:, 2, :], in1=st_sgu[:, 2, :], op=ALU.mult),
             reads=[R("st_sgu")], writes=[R("st_sgu")])
        S.op("dve", lambda e: e.scalar_tensor_tensor(out=st_sgu[:, 4, :], in0=st_sgu[:, 1, :], scalar=1.0 / 128, in1=st_sgu[:, 3, :],
                                                     op0=ALU.mult, op1=ALU.subtract),
             reads=[R("st_sgu"), R("st_s2")], writes=[R("st_sgu")])
        S.op("act", lambda e: e.activation(out=st_sgu[:, 5, :], in_=st_sgu[:, 4, :], func=AF.Ln, bias=epsb[:, 0:1]),
             reads=[R("st_sgu"), R("epsb")], writes=[R("st_sgu")])
        S.op("act", lambda e: e.activation(out=st_sgu[:, 5, :], in_=st_sgu[:, 5, :], func=AF.Exp, scale=-0.5),
             reads=[R("st_sgu")], writes=[R("st_sgu")])
        S.op("dve", lambda e: e.scalar_tensor_tensor(out=st_sgu[:, 6, :], in0=st_sgu[:, 2, :], scalar=-1.0, in1=st_sgu[:, 5, :],
                                                     op0=ALU.mult, op1=ALU.mult),
             reads=[R("st_sgu")], writes=[R("st_sgu")])
        for j in range(4):
            gvt, gvr = chunk[j]["gv"]
            zt, zr = ztm.next()
            for g in range(4):
                S.op("dve", lambda e, g=g: e.tensor_scalar(out=zt[:, g, :], in0=gvt[:, g, :], scalar1=st_sgu[:, 5, j * 4 + g:j * 4 + g + 1],
                                                           scalar2=st_sgu[:, 6, j * 4 + g:j * 4 + g + 1], op0=ALU.mult, op1=ALU.add),
                     reads=[gvr, R("st_sgu")], writes=[zr], disjoint=[zr])
            chunk[j]["z"] = (zt, zr)

        for c in (3, 4):
            wt, wr = slab_get(8 * b + c + 1)
            for h in range(4):
                pb, pr = pbank.next()
                for k in range(8):
                    S.op("pe", lambda e, k=k: e.matmul(pb[:, :], lhsT=wt[:, k, h * 128:(h + 1) * 128],
                                                       rhs=hT[:, k, :], start=(k == 0), stop=(k == 7)),
                         reads=hT_r + [wr], writes=[pr], inc=(k == 7))
                if c == 3:
                    S.op("act", lambda e: e.activation(out=th, in_=pb[:, :], func=AF.Tanh, scale=0.5), reads=[pr], writes=[R("o1")])
                    S.op("dve", lambda e: e.scalar_tensor_tensor(out=th, in0=th, scalar=1.0, in1=pb[:, :], op0=ALU.add, op1=ALU.mult),
                         reads=[R("o1"), pr], writes=[R("o1")])
                    S.op("act", lambda e: e.activation(out=sg[:, h, :], in_=th, func=AF.Identity, scale=gret[:, h:h + 1]),
                         reads=[R("o1"), R("gret")], writes=[sg_r[h]])
                else:
                    S.op("act", lambda e: e.activation(out=gu[:, h, :], in_=pb[:, :], func=AF.Gelu), reads=[pr], writes=[gu_r[h]])
                pe_tick()
            slab_consumed(8 * b + c + 1)
        flush_deferred()
        if b == 3:
            q0_ab[0] = load_quarter_ab(0, guardA=hT_r + [r for _, r in rA.items] + [r for _, r in rB.items],
                                       guardB=[r for _, rr in ktab.items for r in rr] + [R("rotc"), R("rots")])

        wo = [slab_get(8 * b + 6), slab_get(8 * b + 7)]

        if b + 1 < 4:
            phase_A(b + 1)

        mixst = [dict() for _ in range(4)]

        def H1a(j):
            nonlocal cur_sbf
            cj = chunk[j]
            qTt, qTr = cj["qT"]
            kTt, kTr = cj["kT"]
            kt, kr = cj["k"]
            vt, vr = cj["v"]
            catT, catr = catb.next()
            mixst[j]["cat"] = (catT, catr)
            pbs, prs = pbank4.next()
            for h in range(4):
                S.op("pe", lambda e, h=h: e.matmul(pbs[:, h * 128:(h + 1) * 128], lhsT=kTt[:, h, :], rhs=qTt[:, h, :], start=True, stop=True),
                     reads=[kTr, qTr], writes=[prs], inc=(h == 3))
            pbk, prk = pbank4.next()
            for h in range(4):
                S.op("pe", lambda e, h=h: e.matmul(pbk[:, h * 128:(h + 1) * 128], lhsT=kt[:, h, :], rhs=vt[:, h, :], start=True, stop=True),
                     reads=[kr, vr], writes=[prk], inc=(h == 3))
            smt, smr = smb.next()
            S.op("dve", lambda e: e.tensor_tensor(out=smt[:, :, :].rearrange("p h q -> p (h q)"), in0=pbs[:, :],
                                                  in1=maskt[:, :, :].rearrange("p h q -> p (h q)"), op=ALU.mult),
                 reads=[prs, R("mask")], writes=[smr])
            pbank4.release((pbs, prs))
            pbo, pro = pbo2.next()
            sbt, sbr = cur_sbf
            for h in range(4):
                S.op("pe", lambda e, h=h: e.matmul(pbo[:, h * 128:(h + 1) * 128], lhsT=vt[:, h, :], rhs=smt[:, h, :], start=True, stop=False),
                     reads=[vr, smr], writes=[pro], inc=False)
                S.op("pe", lambda e, h=h: e.matmul(pbo[:, h * 128:(h + 1) * 128], lhsT=sbt[:, h, :], rhs=qTt[:, h, :], start=False, stop=True),
                     reads=[sbr, qTr], writes=[pro], inc=(h == 3))
            for h in range(4):
                cd = float((1.0 - 2.0 ** (-5 - h)) ** 128)
                S.op("dve", lambda e, h=h, cd=cd: e.scalar_tensor_tensor(out=Sst[:, h, :], in0=Sst[:, h, :], scalar=cd,
                                                                        in1=pbk[:, h * 128:(h + 1) * 128], op0=ALU.mult, op1=ALU.add),
                     reads=[R("Sst%d" % h), prk], writes=[R("Sst%d" % h)])
            pbank4.release((pbk, prk))
            nsb_t, nsb_r = Sbf.next()
            S.op("act", lambda e: e.activation(out=nsb_t[:, :, :], in_=Sst[:, :, :], func=AF.Copy),
                 reads=[R("Sst%d" % h) for h in range(4)], writes=[nsb_r])
            cur_sbf = (nsb_t, nsb_r)
            mixst[j]["h1"] = (pbo, pro, catT, catr)

        def H1b(j):
            pbo, pro, catT, catr = mixst[j]["h1"]
            sqt, sqr = sqb2.next()
            rst, rsr = rsbB.next()
            S.op("act", lambda e: e.activation(out=sqt[:, :], in_=pbo[:, :], func=AF.Square), reads=[pro], writes=[sqr])
            yield
            pbq, prq = pbank4.next()
            S.op("pe", lambda e: e.matmul(pbq[:, :], lhsT=onesb[:, :], rhs=sqt[:, :], start=True, stop=False),
                 reads=[R("onesb"), sqr], writes=[prq], inc=False)
            S.op("pe", lambda e: e.matmul(pbq[:, :], lhsT=onesb[0:1, :], rhs=epsbf[0:1, :], start=False, stop=True),
                 reads=[R("onesb"), R("epsbf")], writes=[prq])
            yield
            S.op("act", lambda e: e.activation(out=rst[:, :], in_=pbq[:, :], func=AF.Ln, scale=1.0 / 128), reads=[prq], writes=[rsr])
            pbank4.release((pbq, prq))
            S.op("act", lambda e: e.activation(out=rst[:, :], in_=rst[:, :], func=AF.Exp, scale=-0.5), reads=[rsr], writes=[rsr])
            yield
            S.op("pool", lambda e: e.tensor_tensor(out=rst[:, :].rearrange("p (h q) -> p h q", h=4), in0=rst[:, :].rearrange("p (h q) -> p h q", h=4),
                                                   in1=sg[:, :, j * 128:(j + 1) * 128], op=ALU.mult),
                 reads=[rsr] + sg_r, writes=[rsr])
            yield
            S.op("dve", lambda e: e.tensor_tensor(out=catT[:, 0:4, :].rearrange("p h q -> p (h q)"), in0=pbo[:, :], in1=rst[:, :], op=ALU.mult),
                 reads=[pro, rsr], writes=[catr])
            yield

        def H2(j):
            n = nl[j]
            zt, zr = chunk[j]["z"]
            catT, catr = mixst[j]["cat"]
            mxt, mxr = mxB.next()
            if mxr is None:
                mxr = R("mx")
            pbp, prp = pbank4.next()
            for g in range(4):
                S.op("pe", lambda e, g=g: e.matmul(pbp[:, g * 128:(g + 1) * 128], lhsT=zt[:, g, :], rhs=wtm[:, g, :], start=True, stop=True),
                     reads=[zr, R("wtm")], writes=[prp], inc=(g == 3))
            yield
            for g in range(4):
                S.op("dve", lambda e, g=g: e.scalar_tensor_tensor(out=mxt[:, g, :], in0=pbp[:, g * 128:(g + 1) * 128], scalar=lng[:, g:g + 1],
                                                                 in1=bias2[:, g, :], op0=ALU.mult, op1=ALU.add),
                     reads=[prp, R("lng"), R("bias2")], writes=[mxr], disjoint=[mxr])
            pbank4.release((pbp, prp))
            yield
            S.op("pool", lambda e: e.tensor_tensor(out=catT[:, 4:8, :], in0=mxt[:, :, :], in1=gu[:, :, j * 128:(j + 1) * 128], op=ALU.mult),
                 reads=[mxr] + gu_r, writes=[catr])
            yield
            for c in range(2):
                wt, wr = wo[c]
                pb, pr = pbank4.next()
                for f in range(8):
                    S.op("pe", lambda e, f=f: e.matmul(pb[:, :], lhsT=catT[:, f, :], rhs=wt[:, f, :], start=(f == 0), stop=(f == 7)),
                         reads=[catr, wr], writes=[pr], inc=(f == 7))
                S.op("dve", lambda e: e.tensor_tensor(out=X1[:, n, c * 512:(c + 1) * 512], in0=X1[:, n, c * 512:(c + 1) * 512],
                                                      in1=pb[:, :], op=ALU.add),
                     reads=[x1_r[n], pr], writes=[x1_r[n]])
                pbank4.release((pb, pr))
                yield

        def interleave(gens):
            gens = list(gens)
            while gens:
                for g_ in list(gens):
                    try:
                        next(g_)
                    except StopIteration:
                        gens.remove(g_)

        H1a(0)
        H1a(1)
        interleave([H1b(0), H1b(1)])
        H1a(2)
        H1a(3)
        interleave([H2(0), H2(1), H1b(2), H1b(3)])
        interleave([H2(2), H2(3)])
        slab_consumed(8 * b + 6)
        slab_consumed(8 * b + 7)
        stats(nl, ss2, rs2, R("st2"), None, None, x1_r)
        for j in range(4):
            def do_E(j=j, nl=nl):
                n = nl[j]
                norm_to_T(n, j, ss2, rs2, g2c, R("g2c"), H2T, n * 128, h2t_r[n], R("st2"))
            if b < 3 and j >= 2:
                defer(2 * j - 2, do_E)
            else:
                do_E()

    mp1_regs = ([r for _, r in ring1] + [r for _, r in hn.items] + hT_r + [r for _, r in rA.items] + [r for _, r in rB.items]
                + [r for _, r in qtm.items] + [r for _, r in ktm.items] + [r for _, r in vtm.items] + [r for _, r in ztm.items]
                + [r for _, r in qT.items] + [r for _, r in kT.items] + sg_r + gu_r + [r for _, r in smb.items] + [r for _, r in Sbf.items]
                + [r for _, r in catb.items] + [r for _, r in gvb.items] + [r for _, r in sqb2.items] + [r for _, r in rsbB.items] + [r for _, r in mxB.items[1:]] + [r for _, rr in ktab.items for r in rr] + [RG[k] for k in list(RG.keys()) if k not in ("identb", "onesb", "g1c", "g2c", "cw", "cb", "st1", "st2", "epsb")])
    first_mp2 = [True]

    def mp2_guard():
        if first_mp2[0]:
            first_mp2[0] = False
            return mp1_regs
        return []

    S.op("dve", lambda e: e.memset(bar_s[:, 0:1], 0.0), writes=[R("bar_d")] + mp1_regs)
    S.op("act", lambda e: e.activation(out=bar_s[:, 1:2], in_=epsb[:, 0:1], func=AF.Copy), reads=[R("epsb")], writes=[R("bar_a")] + mp1_regs)
    S.op("pool", lambda e: e.memset(bar_s[:, 2:3], 0.0), writes=[R("bar_p")] + mp1_regs)
    S.dma("sp", lambda e: e.dma_start(out=gfb[:, :], in_=gf_d.partition_broadcast(128)), sem_c2, writes=[R("gfb")] + mp1_regs)


    def load_slab2(src_ap, ncols):
        i = ring2_i[0] % 6
        ring2_i[0] += 1
        t, r = ring2[i]
        S.dma("pool", lambda e: e.dma_start(out=t[:, 0:ncols], in_=src_ap), sem_ring2[i], writes=[r] + mp1_regs)
        return t, r

    qw = [None] * 4
    qw[0] = load_quarter_d(0, q0_ab[0])

    def up_steps(q, b, at, aregs):
        (wa, war), (wb_, wbr), (wd, wdr), nf = qw[q]
        f0, f1 = QUARTERS[q]
        wav = wa[:, 0:8 * nf * 128].rearrange("p (k c) -> p k c", k=8)
        wbv = wb_[:, 0:8 * nf * 128].rearrange("p (k c) -> p k c", k=8)
        steps = []
        for fl in range(nf):
            def step(fl=fl):
                fc = f0 + fl
                ys = []
                for ab in range(2):
                    wv, wvr = (wav, war) if ab == 0 else (wbv, wbr)
                    ch = fc if ab == 0 else NFC + fc
                    pb, pr = pbank.next()
                    for k in range(8):
                        S.op("pe", lambda e, k=k: e.matmul(pb[:, :], lhsT=wv[:, k, fl * 128:(fl + 1) * 128],
                                                           rhs=H2T[:, k, b * 512:(b + 1) * 512], start=(k == 0), stop=(k == 7)),
                             reads=[wvr] + h2t_r[4 * b:4 * b + 4], writes=[pr], inc=(k == 7))
                    Ut, Ur = Ub.next()
                    S.op("act", lambda e: e.activation(out=Ut[:, 0:2], in_=halo[:, fl, ab, :], func=AF.Copy), reads=[R("halo")], writes=[Ur])
                    S.op("act", lambda e: e.activation(out=Ut[:, 2:514], in_=pb[:, :], func=AF.Copy), reads=[pr], writes=[Ur], disjoint=[Ur])
                    S.op("act", lambda e: e.activation(out=halo[:, fl, ab, :], in_=pb[:, 510:512], func=AF.Copy), reads=[pr], writes=[R("halo")])
                    Yt, Yr = (Ya.next() if ab == 0 else Yb.next())
                    S.op("act", lambda e: e.activation(out=Yt[:, :], in_=pb[:, :], func=AF.Identity, scale=cw[:, ch, 2:3], bias=cb[:, ch:ch + 1]),
                         reads=[pr, R("cw"), R("cb")], writes=[Yr])
                    S.op("dve", lambda e: e.scalar_tensor_tensor(out=Yt[:, :], in0=Ut[:, 1:513], scalar=cw[:, ch, 1:2], in1=Yt[:, :],
                                                                 op0=ALU.mult, op1=ALU.add),
                         reads=[Ur, Yr, R("cw")], writes=[Yr])
                    S.op("dve", lambda e: e.scalar_tensor_tensor(out=Yt[:, :], in0=Ut[:, 0:512], scalar=cw[:, ch, 0:1], in1=Yt[:, :],
                                                                 op0=ALU.mult, op1=ALU.add),
                         reads=[Ur, Yr, R("cw")], writes=[Yr])
                    ys.append((Yt, Yr))
                Sat, Sar = Sa.next()
                Y0, Y0r = ys[0]
                Y1, Y1r = ys[1]
                S.op("act", lambda e: e.activation(out=Sat[:, :], in_=Y0[:, :], func=AF.Silu), reads=[Y0r], writes=[Sar])
                S.op("pool", lambda e: e.tensor_tensor(out=at[:, fl, :], in0=Sat[:, :], in1=Y1[:, :], op=ALU.mult),
                     reads=[Sar, Y1r], writes=[aregs[fl]])
            steps.append(step)
        return steps

    def down_steps(q, b, at, aregs):
        (wa, war), (wb_, wbr), (wd, wdr), nf = qw[q]
        wdv = wd[:, 0:nf * 1024].rearrange("p (f c) -> p f c", f=nf)
        steps = []
        for j in range(4):
            for c in range(2):
                def step(j=j, c=c):
                    n = 4 * b + j
                    pb, pr = pbank.next()
                    for fl in range(nf):
                        S.op("pe", lambda e, fl=fl: e.matmul(pb[:, :], lhsT=at[:, fl, j * 128:(j + 1) * 128],
                                                             rhs=wdv[:, fl, c * 512:(c + 1) * 512], start=(fl == 0), stop=(fl == nf - 1)),
                             reads=[aregs[fl], wdr], writes=[pr], inc=(fl == nf - 1))
                    S.op("dve", lambda e: e.tensor_tensor(out=X1[:, n, c * 512:(c + 1) * 512], in0=X1[:, n, c * 512:(c + 1) * 512],
                                                          in1=pb[:, :], op=ALU.add),
                         reads=[x1_r[n], pr], writes=[x1_r[n]])
                steps.append(step)
        if q == 3:
            def fin():
                nl = [4 * b + j for j in range(4)]
                stats(nl, ssf, rsf, R("stf"), ostg.items[0][0], ostg.items[0][1], x1_r)
                for n in nl:
                    ot, orl = ostg2.next()
                    orr = orl[0]
                    guard = []
                    if orl[1] is not None:
                        guard = [orl[1]]
                        orl[1] = None
                    S.op("dve", lambda e, n=n: e.scalar_tensor_tensor(out=ot, in0=X1[:, n, :], scalar=rsf[:, n:n + 1], in1=gfb[:, :],
                                                                      op0=ALU.mult, op1=ALU.mult),
                         reads=[x1_r[n], R("stf"), R("gfb")], writes=[orr] + guard)
                    S.dma("sp", lambda e, n=n: e.dma_start(out=out_d[n * 128:(n + 1) * 128, :], in_=ot), orl[2], reads=[orr])
            steps.append(fin)
        return steps

    _stg = []
    for k in range(3):
        for si in range(2):
            rt, rr = ring2[si]
            _stg.append((rt[:, k * 2048:(k + 1) * 2048].bitcast(F32), [Reg("ostg2_%d_%d" % (si, k)), rr, S.new_dma_sem()]))
    ostg2 = Rot(_stg)
    pending_down = []
    for q in range(4):
        S.op("dve", lambda e: e.memset(halo[:, :, :, :], 0.0), writes=[R("halo")])
        for b in range(4):
            at, aregs = actb.next()
            ups = up_steps(q, b, at, aregs)
            downs = pending_down
            nd = len(downs)
            per = -(-nd // max(1, len(ups) - 1)) if nd else 0
            di = 0
            for ui, u in enumerate(ups):
                u()
                if ui >= 1:
                    for _ in range(per):
                        if di < nd:
                            downs[di]()
                            di += 1
            while di < nd:
                downs[di]()
                di += 1
            if b == 0 and q + 1 < 4:
                qw[q + 1] = load_quarter_d(q + 1, load_quarter_ab(q + 1))
            pending_down = down_steps(q, b, at, aregs)
    for d_ in pending_down:
        d_()

    if debug:
        pass
    for _t, _l in ostg2.items:
        S.wait_all_dma("sp", _l[2])
    S.build()
    return nc


_CACHE = {}


def _consts():
    f32 = np.float32
    half = 64
    inv_freq = np.power(f32(10000.0), -np.arange(half, dtype=f32) / f32(half)).astype(f32)
    ang = (np.arange(S_LEN, dtype=f32)[:, None] * inv_freq[None, :]).astype(f32)
    cos = np.cos(ang).astype(f32)
    sin = np.sin(ang).astype(f32)
    C = np.concatenate([cos, cos], -1)
    Sg = np.concatenate([-sin, sin], -1)
    rotc = np.ascontiguousarray(C.reshape(NCH, 128, 128).transpose(1, 0, 2))
    rots = np.ascontiguousarray(Sg.reshape(NCH, 128, 128).transpose(1, 0, 2))
    gam = (1.0 - np.power(2.0, -5.0 - np.arange(4))).astype(np.float64)
    pos = np.arange(128, dtype=np.float64)
    cd = gam ** 128
    kd = (gam[None, :] ** (127.0 - pos)[:, None]) * cd[None, :] * (128.0 ** -0.5)
    kd = np.broadcast_to(kd[:, :, None], (128, 4, 128)).astype(f32).copy()
    tri = (pos[None, :] >= pos[:, None])
    mask = np.where(tri[:, None, :], (1.0 / cd)[None, :, None], 0.0).astype(f32)
    epsr = 128.0 * EPS * (gam[:, None] ** (2.0 * (127.0 - pos))[None, :])
    epsr = np.broadcast_to(epsr[None], (128, 4, 128)).astype(f32).copy()
    tril = np.broadcast_to(tri[:, None, :], (128, 4, 128)).astype(f32).copy()
    ident = np.eye(128, dtype=f32)
    kdh = kd[:, :, 0]
    ckq = np.ascontiguousarray((rotc[:, :, None, :] * kdh[:, None, :, None]).astype(f32))
    skq = np.ascontiguousarray((rots[:, :, None, :] * kdh[:, None, :, None]).astype(f32))
    return dict(rotc=rotc, rots=rots, ckq=ckq, skq=skq, mask=mask, epsr=epsr, tril=tril, ident=ident)


def _col(v, k):
    return np.ascontiguousarray(np.asarray(v, np.float32).reshape(k, 128).T)


def kernel(x, mix_norm_g, w_in, ret_norm_g, sgu_ln_g, sgu_ln_b, sgu_w_s, sgu_b_s,
           w_out, ffn_norm_g, w_up, conv_w, conv_b, w_down, final_norm_g):
    debug = bool(os.environ.get("MK_DEBUG"))
    key = ("nc", debug)
    if key not in _CACHE:
        _CACHE[key] = build_program(debug)
    nc = _CACHE[key]
    f32 = np.float32
    x = np.asarray(x, f32)
    shared = dict(_consts())
    shared.update(
        w_in=np.ascontiguousarray(np.asarray(w_in, f32)[0]),
        w_out=np.ascontiguousarray(np.asarray(w_out, f32)[0]),
        w_up=np.ascontiguousarray(np.asarray(w_up, f32)[0]),
        w_down=np.ascontiguousarray(np.asarray(w_down, f32)[0]),
        g1c=_col(np.asarray(mix_norm_g)[0], 8),
        g2c=_col(np.asarray(ffn_norm_g)[0], 8),
        gf=np.ascontiguousarray(np.asarray(final_norm_g, f32)),
        gret=_col(np.asarray(ret_norm_g)[0], 4),
        lng=np.ascontiguousarray(np.asarray(sgu_ln_g, f32)[0].T),
        lnb=np.ascontiguousarray(np.asarray(sgu_ln_b, f32)[0].T),
        wst=np.ascontiguousarray(np.asarray(sgu_w_s, f32)[0].transpose(2, 0, 1)),
        bs=np.ascontiguousarray(np.asarray(sgu_b_s, f32)[0].reshape(512)),
        cw=np.ascontiguousarray(np.asarray(conv_w, f32)[0].reshape(3, 44, 128).transpose(2, 1, 0)),
        cb=_col(np.asarray(conv_b)[0], 44),
    )
    in_maps = []
    for c in range(8):
        m = dict(shared)
        m["x"] = np.ascontiguousarray(x[c])
        in_maps.append(m)
    res = run_bass_kernel_spmd(nc, in_maps, core_ids=list(range(8)))
    out = np.stack([np.asarray(r["out"], f32) for r in res.results], axis=0)
    if debug:
        kernel.last_dbg = [r.get("dbg") for r in res.results]
    return out
```
